# Optimizing a Trainium2 kernel written in Bass

```python
import jax, jax.numpy as jnp
from jax import lax
import numpy as np

D_MODEL = 2048
BATCH = 2
SEQ = 4096
DEPTH = 1

ATTN_HEADS = 8
HEAD_DIM = 128
ATTN_WIDTH = ATTN_HEADS * HEAD_DIM
CONV_WIDTH = D_MODEL // 2
CONV_K = 3
MOBA_BLOCK = 256
MOBA_TOP_K = 3
QUERY_CHUNK = 64
D_FF = 4 * D_MODEL
PLE_DIM = 256
RMS_EPS = 1e-6
IN_SIZES = (ATTN_WIDTH, ATTN_WIDTH, ATTN_WIDTH, CONV_WIDTH, CONV_WIDTH, CONV_WIDTH, D_MODEL, D_MODEL)
IN_COLS = sum(IN_SIZES)

kernel_name = 'hybrid_moba_shortconv_block'


def rms_norm(x, g):
    xf = x.astype(jnp.float32)
    y = xf * lax.rsqrt(jnp.mean(xf * xf, axis=-1, keepdims=True) + RMS_EPS)
    return (y * g.astype(jnp.float32)).astype(x.dtype)


def moba_attention(q, k, v):
    B, H, S, hd = q.shape
    dt = q.dtype
    f32 = jnp.float32
    nb = -(-S // MOBA_BLOCK)
    pad = nb * MOBA_BLOCK - S
    kb = jnp.pad(k, ((0, 0), (0, 0), (0, pad), (0, 0))).reshape(B, H, nb, MOBA_BLOCK, hd)
    vb = jnp.pad(v, ((0, 0), (0, 0), (0, pad), (0, 0))).reshape(B, H, nb, MOBA_BLOCK, hd)
    q_blk = jnp.arange(S) // MOBA_BLOCK
    n_sel = min(MOBA_TOP_K, nb - 1)
    scale = HEAD_DIM ** -0.5
    if n_sel > 0:
        k_mean = jnp.mean(kb.astype(f32), axis=3)
        gate = jnp.einsum('bhsd,bhnd->bhsn', q.astype(f32), k_mean)
        past = jnp.arange(nb)[None, :] < q_blk[:, None]
        gate = jnp.where(past[None, None], gate, -jnp.inf)
        _, sel = lax.top_k(gate, n_sel)
        sel_valid = jnp.arange(n_sel)[None, :] < q_blk[:, None]
    bi = jnp.arange(B)[:, None, None, None]
    hi = jnp.arange(H)[None, :, None, None]

    def chunk(c):
        q0 = c * QUERY_CHUNK
        qc = lax.dynamic_slice_in_dim(q, q0, QUERY_CHUNK, axis=2)
        qpos = q0 + jnp.arange(QUERY_CHUNK)
        blk = q0 // MOBA_BLOCK
        k_own = lax.dynamic_index_in_dim(kb, blk, axis=2, keepdims=False)
        v_own = lax.dynamic_index_in_dim(vb, blk, axis=2, keepdims=False)
        kpos = blk * MOBA_BLOCK + jnp.arange(MOBA_BLOCK)
        s_own = jnp.einsum('bhqd,bhkd->bhqk', qc, k_own, preferred_element_type=f32) * scale
        s_own = jnp.where((kpos[None, :] <= qpos[:, None])[None, None], s_own, -jnp.inf)
        if n_sel > 0:
            sel_c = lax.dynamic_slice_in_dim(sel, q0, QUERY_CHUNK, axis=2)
            val_c = lax.dynamic_slice_in_dim(sel_valid, q0, QUERY_CHUNK, axis=0)
            k_sel = kb[bi, hi, sel_c]
            v_sel = vb[bi, hi, sel_c]
            s_sel = jnp.einsum('bhqd,bhqnkd->bhqnk', qc, k_sel, preferred_element_type=f32) * scale
            s_sel = jnp.where(val_c[None, None, :, :, None], s_sel, -jnp.inf)
            s = jnp.concatenate([s_sel.reshape(B, H, QUERY_CHUNK, n_sel * MOBA_BLOCK), s_own], axis=-1)
            probs = jax.nn.softmax(s, axis=-1)
            p_sel = probs[..., :n_sel * MOBA_BLOCK].reshape(B, H, QUERY_CHUNK, n_sel, MOBA_BLOCK)
            p_own = probs[..., n_sel * MOBA_BLOCK:]
            o = (jnp.einsum('bhqnk,bhqnkd->bhqd', p_sel.astype(dt), v_sel, preferred_element_type=f32)
                 + jnp.einsum('bhqk,bhkd->bhqd', p_own.astype(dt), v_own, preferred_element_type=f32))
        else:
            probs = jax.nn.softmax(s_own, axis=-1)
            o = jnp.einsum('bhqk,bhkd->bhqd', probs.astype(dt), v_own, preferred_element_type=f32)
        return o.astype(dt)

    out = lax.map(chunk, jnp.arange(S // QUERY_CHUNK))
    return out.transpose(1, 2, 0, 3, 4).reshape(B, H, S, hd)


def causal_depthwise_conv(u, w):
    C = u.shape[-1]
    return lax.conv_general_dilated(
        u, w[:, None, :].astype(u.dtype), window_strides=(1,), padding=[(CONV_K - 1, 0)],
        dimension_numbers=('NWC', 'WIO', 'NWC'), feature_group_count=C)


def setup_inputs(seed: int = 0) -> dict:
    key = jax.random.key(seed)
    ks = jax.random.split(key, 16)
    f32 = jnp.float32

    def nrm(k, shape, scale):
        return jax.random.normal(k, shape, f32) * scale

    def gain(k, shape):
        return 1.0 + 0.05 * jax.random.normal(k, shape, f32)

    return {
        'x': nrm(ks[0], (BATCH, SEQ, D_MODEL), 1.0),
        'p': nrm(ks[1], (DEPTH, BATCH, SEQ, PLE_DIM), 1.0),
        'g_mix': gain(ks[2], (DEPTH, D_MODEL)),
        'w_in': nrm(ks[3], (DEPTH, D_MODEL, IN_COLS), D_MODEL ** -0.5),
        'g_q': gain(ks[4], (DEPTH, HEAD_DIM)),
        'g_k': gain(ks[5], (DEPTH, HEAD_DIM)),
        'w_conv': nrm(ks[6], (DEPTH, CONV_K, CONV_WIDTH), CONV_K ** -0.5),
        'w_attn_out': nrm(ks[7], (DEPTH, ATTN_WIDTH, D_MODEL), ATTN_WIDTH ** -0.5),
        'w_conv_out': nrm(ks[8], (DEPTH, CONV_WIDTH, D_MODEL), CONV_WIDTH ** -0.5),
        'w_o': nrm(ks[9], (DEPTH, D_MODEL, D_MODEL), D_MODEL ** -0.5),
        'g_mlp': gain(ks[10], (DEPTH, D_MODEL)),
        'w_up': nrm(ks[11], (DEPTH, D_MODEL, D_FF), D_MODEL ** -0.5),
        'w_down': nrm(ks[12], (DEPTH, D_FF, D_MODEL), D_FF ** -0.5),
        'g_ple': gain(ks[13], (DEPTH, D_MODEL)),
        'w_ple_gate': nrm(ks[14], (DEPTH, D_MODEL, D_MODEL), D_MODEL ** -0.5),
        'w_ple_proj': nrm(ks[15], (DEPTH, PLE_DIM, D_MODEL), PLE_DIM ** -0.5),
    }


def reference(x, p, g_mix, w_in, g_q, g_k, w_conv, w_attn_out, w_conv_out, w_o,
              g_mlp, w_up, w_down, g_ple, w_ple_gate, w_ple_proj):
    B, S, D = x.shape
    cuts = [int(c) for c in np.cumsum(IN_SIZES)[:-1]]
    r = x
    for i in range(DEPTH):
        h = rms_norm(r, g_mix[i])
        z = h @ w_in[i]
        q, k, v, c_b, c_c, c_x, g_a, g_c = jnp.split(z, cuts, axis=-1)
        q = rms_norm(q.reshape(B, S, ATTN_HEADS, HEAD_DIM), g_q[i]).transpose(0, 2, 1, 3)
        k = rms_norm(k.reshape(B, S, ATTN_HEADS, HEAD_DIM), g_k[i]).transpose(0, 2, 1, 3)
        v = v.reshape(B, S, ATTN_HEADS, HEAD_DIM).transpose(0, 2, 1, 3)
        attn = moba_attention(q, k, v)
        y_attn = attn.transpose(0, 2, 1, 3).reshape(B, S, ATTN_WIDTH) @ w_attn_out[i]
        y_conv = (c_b * causal_depthwise_conv(c_c * c_x, w_conv[i])) @ w_conv_out[i]
        merged = jax.nn.sigmoid(g_a) * y_attn + jax.nn.sigmoid(g_c) * y_conv
        r = r + merged @ w_o[i]
        h = rms_norm(r, g_mlp[i])
        r = r + jnp.square(jax.nn.relu(h @ w_up[i])) @ w_down[i]
        h = rms_norm(r, g_ple[i])
        r = r + jax.nn.sigmoid(h @ w_ple_gate[i]) * (p[i] @ w_ple_proj[i])
    return r
```

```python
import os
from contextlib import ExitStack

import numpy as np
import ml_dtypes

import concourse.bass as bass
import concourse.mybir as mybir
from concourse.bass_utils import run_bass_kernel_spmd

F32 = mybir.dt.float32
BF16 = mybir.dt.bfloat16
AF = mybir.ActivationFunctionType
ALU = mybir.AluOpType
AX = mybir.AxisListType

D = 2048
KC = D // 128
S = 4096
NBLK = 16
BLK = 256
H = 8
HD = 128
DFF = 8192
PLE = 256
EPS = 1e-6
T = 1024
NEG = -30000.0
BIG = 1.0e30

DEBUG = os.environ.get("MK_DEBUG", "")


class Sem:
    def __init__(self, h, name):
        self.h = h
        self.v = 0
        self.name = name


class Buf:
    __slots__ = ("name", "w", "rd", "excl")

    def __init__(self, name, excl=False):
        self.name = name
        self.w = None
        self.rd = {}
        self.excl = excl


class Eng:
    def __init__(self, eng, sem, inorder=False):
        self.eng = eng
        self.sem = sem
        self.waited = {}
        self.inorder = inorder


class Arena:
    DT_SIZE = {F32: 4, BF16: 2}

    def __init__(self, nc, lo, hi):
        self.nc = nc
        self.free_list = [(lo, hi)]
        self.live = {}
        self.uid = 0

    def alloc(self, name, shape, dt=F32, top=False):
        n = 1
        for d in shape[1:]:
            n *= d
        size = (n * self.DT_SIZE[dt] + 63) // 64 * 64
        order = list(enumerate(self.free_list))
        if top:
            order = order[::-1]
        for i, (lo, hi) in order:
            if hi - lo >= size:
                if top:
                    off = hi - size
                    if hi - lo == size:
                        self.free_list.pop(i)
                    else:
                        self.free_list[i] = (lo, hi - size)
                else:
                    off = lo
                    if hi - lo == size:
                        self.free_list.pop(i)
                    else:
                        self.free_list[i] = (lo + size, hi)
                break
        else:
            raise MemoryError(f"SBUF arena full allocating {name} {size}: free={self.free_list} live={self.live}")
        self.uid += 1
        t = self.nc.alloc_sbuf_tensor_at(f"sb_{name}_{self.uid}", list(shape), dt, offset=off)
        self.live[name] = (off, size)
        return t

    def free(self, *names):
        for name in names:
            off, size = self.live.pop(name)
            self.free_list.append((off, off + size))
        self.free_list.sort()
        merged = []
        for lo, hi in self.free_list:
            if merged and merged[-1][1] == lo:
                merged[-1] = (merged[-1][0], hi)
            else:
                merged.append((lo, hi))
        self.free_list = merged


class K:
    def __init__(self, nc, es):
        self.nc = nc
        self.es = es
        self.nsem = 0
        self.PE = Eng(nc.tensor, self.new_sem("pe"), inorder=True)
        self.ACT = Eng(nc.scalar, self.new_sem("act"))
        self.DVE = Eng(nc.vector, self.new_sem("dve"))
        self.POOL = Eng(nc.gpsimd, self.new_sem("pool"))
        self.SP = Eng(nc.sync, self.new_sem("sp"))
        self.engines = [self.PE, self.ACT, self.DVE, self.POOL, self.SP]
        self.all_sems = []
        self.arena = Arena(nc, 16640, 229344 - 64)

    def new_sem(self, name):
        h = self.es.enter_context(self.nc.semaphore(f"s_{name}_{self.nsem}"))
        self.nsem += 1
        s = Sem(h, name)
        if hasattr(self, "all_sems"):
            self.all_sems.append(s)
        return s

    def _deps(self, rd, wr):
        deps = {}

        def add(tok):
            if tok is None:
                return
            s, v = tok
            if deps.get(s, 0) < v:
                deps[s] = v

        for b in rd:
            add(b.w)
            if b.excl:
                for s, v in b.rd.items():
                    add((s, v))
        for b in wr:
            add(b.w)
            for s, v in b.rd.items():
                add((s, v))
        return deps

    def _wait(self, E, deps):
        for s, v in deps.items():
            if v <= 0 or E.waited.get(s, 0) >= v:
                continue
            if s is E.sem and E.inorder:
                continue
            E.eng.wait_ge(s.h, v)
            E.waited[s] = v

    def op(self, E, fn, rd=(), wr=()):
        self._wait(E, self._deps(rd, wr))
        inst = fn()
        E.sem.v += 1
        inst.then_inc(E.sem.h, 1)
        tok = (E.sem, E.sem.v)
        for b in wr:
            b.w = tok
            b.rd = {}
        for b in rd:
            if b.rd.get(E.sem, 0) < E.sem.v:
                b.rd[E.sem] = E.sem.v
        return inst

    def dma(self, Q, sem, out, in_, rd=(), wr=()):
        deps = self._deps(rd, wr)
        if sem.v > 0 and deps.get(sem, 0) < sem.v:
            deps[sem] = sem.v
        self._wait(Q, deps)
        inst = Q.eng.dma_start(out=out, in_=in_)
        sem.v += 16
        inst.then_inc(sem.h, 16)
        tok = (sem, sem.v)
        for b in wr:
            b.w = tok
            b.rd = {}
        for b in rd:
            if b.rd.get(sem, 0) < sem.v:
                b.rd[sem] = sem.v
        return inst

    def barrier(self):
        sems = [e.sem for e in self.engines] + self.all_sems
        for E in self.engines:
            deps = {s: s.v for s in sems if s.v > 0}
            self._wait(E, deps)

    def finish(self, sems):
        deps = {s: s.v for s in sems if s.v > 0}
        self._wait(self.SP, deps)


class WStream:
    SLOT_ELEMS = 16 * 512

    def __init__(self, k, nslots):
        self.k = k
        self.n = nslots
        self.t = k.arena.alloc("wring", [128, nslots, self.SLOT_ELEMS], BF16)
        self.bufs = [Buf(f"wslot{i}") for i in range(nslots)]
        self.sems = [k.new_sem(f"w{i}") for i in range(nslots)]
        self.plan = []
        self.issued = 0
        self.acquired = 0
        self.released = 0

    def set_plan(self, plan):
        self.plan = plan

    def _issue(self):
        i = self.issued
        if i >= len(self.plan):
            return
        key, src, nkc, ncols = self.plan[i]
        s = i % self.n
        dst = self.t[:, s, 0:nkc * ncols].rearrange("p (k c) -> p k c", k=nkc)
        self.k.dma(self.k.POOL, self.sems[s], dst, src.rearrange("(k p) c -> p k c", p=128), wr=[self.bufs[s]])
        self.issued += 1

    def start(self):
        self._issue()
        self.k._wait(self.k.POOL, {self.sems[0]: self.sems[0].v})
        while self.issued < min(self.n, len(self.plan)):
            self._issue()

    def acquire(self, key):
        i = self.acquired
        pk, src, nkc, ncols = self.plan[i]
        assert pk == key, (pk, key, i)
        assert i < self.issued, "weight not issued"
        s = i % self.n
        self.acquired += 1
        ap = self.t[:, s, 0:nkc * ncols].rearrange("p (k c) -> p k c", k=nkc)
        return self.bufs[s], ap

    def release(self):
        self.released += 1
        while self.issued < min(self.released + self.n, len(self.plan)):
            self._issue()


def core_perm(j):
    own = [j, 7 - j, 8 + j, 15 - j]
    perm = []
    for s in range(4):
        grp = [4 * s + i for i in range(4)]
        others = [g for g in grp if g != own[s]]
        perm += others + [own[s]]
    return own, perm


def build_program(dbg=""):
    nc = bass.Bass("TRN2", target_bir_lowering=False)
    es = ExitStack()
    k = K(nc, es)

    needed = None
    if dbg and dbg[0] in "0A":
        needed = {"x_all", "w_in", "gains", "gqk", "pastb", "ident", "esel", "tri", "w_conv"}
    if dbg and dbg[0] == "B":
        needed = {"x_all", "x_halo", "w_in", "gains", "gqk", "pastb", "ident", "esel", "tri", "w_conv", "w_ao", "w_co", "w_o"}
    declared = []

    def din(name, shape, dt=F32):
        if needed is not None and name not in needed:
            return None
        declared.append(name)
        return nc.dram_tensor(name, list(shape), dt, kind="ExternalInput").ap()

    x_all = din("x_all", [S, D])
    x_halo = din("x_halo", [8, D])
    p_own = din("p_own", [T, PLE])
    w_in = din("w_in", [D, 10240])
    w_conv = din("w_conv", [128, 8, 3])
    w_ao = din("w_ao", [1024, D])
    w_co = din("w_co", [1024, D])
    w_o = din("w_o", [D, D])
    w_up = din("w_up", [D, DFF])
    w_down = din("w_down", [DFF, D])
    w_pg = din("w_pg", [D, D])
    w_pp = din("w_pp", [PLE, D])
    gains = din("gains", [128, 3, KC])
    gqk = din("gqk", [128, 2])
    pastb = din("pastb", [128, 2, 8, 16])
    ident_in = din("ident", [128, 128])
    esel_in = din("esel", [128, 16 * 128], BF16)
    tri_in = din("tri", [128, 2, 256], BF16)
    out = nc.dram_tensor("out", [T, D], F32, kind="ExternalOutput").ap()
    kt_s = nc.dram_tensor("kt_scratch", [H, 128, S], BF16, kind="Internal").ap()
    v_s = nc.dram_tensor("v_scratch", [S, H * HD], BF16, kind="Internal").ap()
    dbg_out = None
    if dbg:
        dbg_out = nc.dram_tensor("dbg", [128, 16384], F32, kind="ExternalOutput").ap()

    PE, ACT, DVE, POOL, SP = k.PE, k.ACT, k.DVE, k.POOL, k.SP

    def sb(name, shape, dt=F32):
        return k.arena.alloc(name, list(shape), dt)

    ident = sb("ident", [128, 128]);            b_ident = Buf("ident")
    ident_bf = sb("ident_bf", [128, 128], BF16); b_ident_bf = Buf("ident_bf")
    ones_bf = sb("ones_bf", [128, 128], BF16);  b_ones = Buf("ones")
    gains_sb = sb("gains", [128, 3, KC]);       b_gains = Buf("gains")
    gqk_sb = sb("gqk", [128, 2]);               b_gqk = Buf("gqk")
    pastb_sb = sb("pastb", [128, 2, 8, 16]);    b_pastb = Buf("pastb")
    wconv_sb = sb("wconv", [128, 8, 3]);        b_wconv = Buf("wconv")
    kmean = sb("kmean", [128, H, NBLK]);        b_kmean = [Buf(f"kmean{c}") for c in range(8)]
    kmean_bf = sb("kmean_bf", [128, H, NBLK], BF16); b_kmean_bf = Buf("kmean_bf")
    epsb = sb("epsb", [128, 1]);                b_eps = Buf("eps")

    psum = es.enter_context(nc.psum_tensor("psum", [128, 8, 512], F32))
    bank = [Buf(f"bank{i}", excl=True) for i in range(8)]

    c_sem = k.new_sem("const")

    def load_const(dst, src, b):
        k.dma(SP, k.new_sem("c"), dst, src, wr=[b])

    load_const(ident[:], ident_in[:], b_ident)
    load_const(gains_sb[:], gains[:], b_gains)
    load_const(gqk_sb[:], gqk[:], b_gqk)
    load_const(pastb_sb[:], pastb[:], b_pastb)
    load_const(wconv_sb[:], w_conv[:], b_wconv)
    k.op(DVE, lambda: nc.vector.memset(ones_bf[:], 1.0), wr=[b_ones])
    k.op(DVE, lambda: nc.vector.memset(epsb[:], EPS), wr=[b_eps])
    k.op(DVE, lambda: nc.vector.tensor_copy(out=ident_bf[:], in_=ident[:]), rd=[b_ident], wr=[b_ident_bf])

    ws = WStream(k, 3)
    h1T = sb("h1T", [128, KC, T], BF16);        b_h1T = [Buf(f"h1T{i}") for i in range(KC)]

    def wsrc(wap, r0, r1, c0, c1):
        return wap[r0:r1, c0:c1]

    plan = []
    plan.append(("wk0", wsrc(w_in, 0, D, 1024, 1536), 16, 512))
    plan.append(("wk1", wsrc(w_in, 0, D, 1536, 2048), 16, 512))
    plan.append(("wv0", wsrc(w_in, 0, D, 2048, 2560), 16, 512))
    if not (dbg and dbg[0] in "0A"):
        plan.append(("wq0", wsrc(w_in, 0, D, 0, 512), 16, 512))
        plan.append(("wq1", wsrc(w_in, 0, D, 512, 1024), 16, 512))
        for hf in range(2):
            plan.append((f"wcc{hf}", wsrc(w_in, 0, D, 4096 + hf * 512, 4096 + hf * 512 + 512), 16, 512))
            plan.append((f"wcx{hf}", wsrc(w_in, 0, D, 5120 + hf * 512, 5120 + hf * 512 + 512), 16, 512))
            plan.append((f"wcb{hf}", wsrc(w_in, 0, D, 3072 + hf * 512, 3072 + hf * 512 + 512), 16, 512))
        for cb in range(4):
            plan.append((f"wgc{cb}", wsrc(w_in, 0, D, 8192 + cb * 512, 8192 + cb * 512 + 512), 16, 512))
            plan.append((f"wco{cb}", wsrc(w_co, 0, 1024, cb * 512, cb * 512 + 512), 8, 512))
            plan.append((f"wga{cb}", wsrc(w_in, 0, D, 6144 + cb * 512, 6144 + cb * 512 + 512), 16, 512))
            plan.append((f"wao{cb}", wsrc(w_ao, 0, 1024, cb * 512, cb * 512 + 512), 8, 512))
        for cb in range(4):
            plan.append((f"wo{cb}", wsrc(w_o, 0, D, cb * 512, cb * 512 + 512), 16, 512))
        if not (dbg and dbg[0] == "B"):
            for j in range(8):
                for cbh in range(2):
                    c0 = j * 1024 + cbh * 512
                    plan.append((f"wup{j}_{cbh}", wsrc(w_up, 0, D, c0, c0 + 512), 16, 512))
                for cbh in range(2):
                    plan.append((f"wdn{j}_{cbh}", wsrc(w_down, j * 1024, (j + 1) * 1024, cbh * 1024, cbh * 1024 + 1024), 8, 1024))
            for cb in range(4):
                plan.append((f"wpg{cb}", wsrc(w_pg, 0, D, cb * 512, cb * 512 + 512), 16, 512))
    ws.set_plan(plan)
    ws.start()

    st = dict(nc=nc, k=k, es=es, ws=ws, psum=psum, bank=bank, dbg=dbg, dbg_out=dbg_out)
    st.update(locals())
    if dbg != "0":
        phase_a(st)
    if not (dbg and dbg[0] in "0A"):
        st.update(locals())
        phase_rest(st)

    k.barrier()
    es.close()
    nc._mk_declared = declared
    return nc


def phase_a(st):
    nc = st["nc"]; k = st["k"]; ws = st["ws"]; psum = st["psum"]; bank = st["bank"]
    PE, ACT, DVE, POOL, SP = k.PE, k.ACT, k.DVE, k.POOL, k.SP
    x_all = st["x_all"]; w_in = st["w_in"]; kt_s = st["kt_s"]; v_s = st["v_s"]
    ident = st["ident"]; b_ident = st["b_ident"]; ones_bf = st["ones_bf"]; b_ones = st["b_ones"]
    gains_sb = st["gains_sb"]; b_gains = st["b_gains"]; gqk_sb = st["gqk_sb"]; b_gqk = st["b_gqk"]
    kmean = st["kmean"]; b_kmean = st["b_kmean"]; epsb = st["epsb"]; b_eps = st["b_eps"]
    dbg = st["dbg"]; dbg_out = st["dbg_out"]
    h1T = st["h1T"]; b_h1T = st["b_h1T"]

    if True:
        local_names = []

        def sb(name, shape, dt=F32):
            local_names.append(name)
            return k.arena.alloc(name, list(shape), dt)

        NXT = 4
        xt = sb("xt", [128, NXT, D]);              b_xt = [Buf(f"xt{i}") for i in range(NXT)]
        s_xt = [k.new_sem(f"xt{i}") for i in range(NXT)]
        xTc = sb("xTc", [128, KC, 512]);           b_xTc = [Buf(f"xTc{i}") for i in range(KC)]
        hTc = sb("hTc", [128, KC, 512], BF16);     b_hTc = [Buf(f"hTc{i}") for i in range(KC)]
        wv1 = sb("wv1", [128, KC, 512], BF16);     b_wv1 = Buf("wv1")
        sq = sb("sq", [128, 4, 512], BF16);        b_sq = [Buf(f"sq{i}") for i in range(4)]
        rs = sb("rs", [128, 2, 512]);              b_rs = [Buf("rs0"), Buf("rs1")]
        kst = sb("kst", [128, 2, 512], BF16);      b_kst = [Buf("kst0"), Buf("kst1")]
        s_kst = [k.new_sem("kst0"), k.new_sem("kst1")]
        vst = sb("vst", [128, 1, 4, 1024], BF16);  b_vst = [Buf("vst0")]
        s_vst = [k.new_sem("vst0")]

        k.dma(POOL, k.new_sem("wv1"), wv1[:], w_in[:, 2560:3072].rearrange("(k p) c -> p k c", p=128), wr=[b_wv1])

        b_wk0, wk0 = ws.acquire("wk0")
        b_wk1, wk1 = ws.acquire("wk1")
        b_wv0, wv0 = ws.acquire("wv0")

        xtile_i = [0]

        def load_xtile(tile_idx):
            i = tile_idx % NXT
            k.dma(SP, s_xt[i], xt[:, i, :], x_all[tile_idx * 128:(tile_idx + 1) * 128, :], wr=[b_xt[i]])

        NT = S // 128
        for t0 in range(NXT):
            load_xtile(t0)
        nl = [NXT]
        ginv = sb("ginv", [128, KC]); b_ginv = Buf("ginv")
        k.op(DVE, lambda: nc.vector.reciprocal(out=ginv[:], in_=gains_sb[:, 0, :]), rd=[b_gains], wr=[b_ginv])
        sqc = [0]
        nchunks = 1 if (dbg and dbg[0] == "A" and len(dbg) > 1) else S // 512

        NSQ = 4
        rot = [3, 4, 6, 7]
        roti = [0]
        sq_of = {}

        def emit_Ttrans(c, kc):
            bk = kc % 2
            for t in range(4):
                ti = c * 4 + t
                xi = ti % NXT
                k.op(PE, lambda t=t, xi=xi: nc.tensor.transpose(
                    psum[:, bk, t * 128:(t + 1) * 128], xt[:, xi, kc * 128:(kc + 1) * 128], ident[:]),
                    rd=[b_xt[xi], b_ident], wr=[bank[bk]])
            k.op(DVE, lambda: nc.vector.tensor_scalar(out=xTc[:, kc, :], in0=psum[:, bk, :], scalar1=gains_sb[:, 0, kc:kc + 1],
                                                      scalar2=None, op0=ALU.mult),
                 rd=[bank[bk], b_gains], wr=[b_xTc[kc]])
            si = sqc[0] % NSQ
            sqc[0] += 1
            sq_of[("n", c, kc)] = si
            k.op(ACT, lambda: nc.scalar.activation(out=sq[:, si, :], in_=xTc[:, kc, :], func=AF.Square, scale=ginv[:, kc:kc + 1]),
                 rd=[b_xTc[kc], b_ginv], wr=[b_sq[si]])

        def emit_Tsum(c, kc):
            si = sq_of.pop(("n", c, kc))
            k.op(PE, lambda: nc.tensor.matmul(psum[:, 2, :], lhsT=ones_bf[:], rhs=sq[:, si, :], start=(kc == 0), stop=(kc == KC - 1)),
                 rd=[b_ones, b_sq[si]], wr=[bank[2]])
            if kc == KC - 1:
                for t in range(4):
                    if nl[0] < NT:
                        load_xtile(nl[0])
                        nl[0] += 1
                k.op(ACT, lambda: nc.scalar.activation(out=rs[:, 0, :], in_=psum[:, 2, :], func=AF.Sqrt, bias=epsb[:], scale=1.0 / D),
                     rd=[bank[2], b_eps], wr=[b_rs[0]])
                k.op(DVE, lambda: nc.vector.reciprocal(out=rs[:, 0, :], in_=rs[:, 0, :]), rd=[b_rs[0]], wr=[b_rs[0]])

        b_hTa = [Buf(f"hTa{i}") for i in range(KC)]

        def hbuf(c):
            if c % 2 == 0 and c <= 4:
                return (lambda kc: h1T[:, kc, 512:1024]), b_hTa
            return (lambda kc: hTc[:, kc, :]), b_hTc

        def emit_hT(c):
            hf, hb_ = hbuf(c)
            for kc in range(KC):
                if kc % 3 != 2:
                    k.op(DVE, lambda kc=kc: nc.vector.tensor_tensor(out=hf(kc), in0=xTc[:, kc, :], in1=rs[:, 0, :], op=ALU.mult),
                         rd=[b_xTc[kc], b_rs[0]], wr=[hb_[kc]])
                else:
                    k.op(POOL, lambda kc=kc: nc.gpsimd.tensor_tensor(out=hf(kc), in0=xTc[:, kc, :], in1=rs[:, 0, :], op=ALU.mult),
                         rd=[b_xTc[kc], b_rs[0]], wr=[hb_[kc]])
            if c % 2 == 1:
                so = (c - 1) // 2
                for kc in range(KC):
                    k.op(POOL, lambda kc=kc, so=so: nc.gpsimd.tensor_copy(out=h1T[:, kc, so * 256:(so + 1) * 256], in_=hTc[:, kc, 256:512]),
                         rd=[b_hTc[kc]], wr=[b_h1T[kc], b_hTa[kc]])

        kbank = {}

        def emit_Kproj(c, h):
            wb, wt = (b_wk0, wk0) if h < 4 else (b_wk1, wk1)
            hc = (h % 4) * 128
            pb = rot[roti[0] % 4]
            roti[0] += 1
            kbank[h] = pb
            hf, hb_ = hbuf(c)
            for g in range(4):
                def kproj(g=g):
                    for kc in range(4 * g, 4 * g + 4):
                        inst = nc.tensor.matmul(psum[:, pb, :], lhsT=wt[:, kc, hc:hc + 128], rhs=hf(kc),
                                                start=(kc == 0), stop=(kc == KC - 1))
                    return inst
                k.op(PE, kproj, rd=[wb] + hb_[4 * g:4 * g + 4], wr=[bank[pb]])
            si = sqc[0] % NSQ
            sqc[0] += 1
            sq_of[("k", c, h)] = si
            k.op(ACT, lambda: nc.scalar.activation(out=sq[:, si, :], in_=psum[:, pb, :], func=AF.Square), rd=[bank[pb]], wr=[b_sq[si]])

        def emit_Kfin(c, h):
            pb = kbank[h]
            si = sq_of.pop(("k", c, h))
            k.op(PE, lambda: nc.tensor.matmul(psum[:, 5, :], lhsT=ones_bf[:], rhs=sq[:, si, :], start=True, stop=True),
                 rd=[b_ones, b_sq[si]], wr=[bank[5]])
            k.op(ACT, lambda: nc.scalar.activation(out=rs[:, 1, :], in_=psum[:, 5, :], func=AF.Sqrt, bias=epsb[:], scale=1.0 / HD),
                 rd=[bank[5], b_eps], wr=[b_rs[1]])
            k.op(DVE, lambda: nc.vector.reciprocal(out=rs[:, 1, :], in_=rs[:, 1, :]), rd=[b_rs[1]], wr=[b_rs[1]])
            ki = h % 2
            k.op(DVE, lambda: nc.vector.scalar_tensor_tensor(out=kst[:, ki, :], in0=psum[:, pb, :], scalar=gqk_sb[:, 1:2], in1=rs[:, 1, :],
                                                             op0=ALU.mult, op1=ALU.mult),
                 rd=[bank[pb], b_gqk, b_rs[1]], wr=[b_kst[ki]])
            k.op(DVE, lambda: nc.vector.tensor_reduce(out=kmean[:, h, 2 * c:2 * c + 2], in_=kst[:, ki, :].rearrange("p (b t) -> p b t", b=2),
                                                      op=ALU.add, axis=AX.X),
                 rd=[b_kst[ki]], wr=[b_kmean[c]])
            k.dma(SP, s_kst[ki], kt_s[h, :, c * 512:(c + 1) * 512], kst[:, ki, :], rd=[b_kst[ki]])

        def emit_V(c, t, cb):
            wb, wt = (b_wv0, wv0) if cb == 0 else (b_wv1, wv1)
            pb = rot[roti[0] % 4]
            roti[0] += 1

            hf, hb_ = hbuf(c)

            def vproj():
                for kc in range(KC):
                    inst = nc.tensor.matmul(psum[:, pb, :], lhsT=hf(kc)[:, t * 128:(t + 1) * 128], rhs=wt[:, kc, :],
                                            start=(kc == 0), stop=(kc == KC - 1))
                return inst
            k.op(PE, vproj, rd=[wb] + hb_, wr=[bank[pb]])
            k.op(ACT, lambda: nc.scalar.copy(out=vst[:, 0, t, cb * 512:(cb + 1) * 512], in_=psum[:, pb, :]), rd=[bank[pb]], wr=[b_vst[0]])
            if t == 3 and cb == 1:
                k.dma(SP, s_vst[0], v_s[c * 512:(c + 1) * 512, :].rearrange("(t p) c -> p t c", p=128), vst[:, 0, :, :], rd=[b_vst[0]])

        early_release = not dbg
        for kc in range(KC):
            emit_Ttrans(0, kc)
            emit_Tsum(0, kc)
        emit_hT(0)
        for c in range(nchunks):
            nxt = c + 1 < nchunks
            for i in range(KC):
                if dbg != "A1":
                    if i < H:
                        emit_Kproj(c, i)
                    elif dbg != "A2":
                        emit_V(c, (i - H) // 2, (i - H) % 2)
                if early_release and c == nchunks - 1 and i in (3, 7, 14):
                    ws.release()
                if nxt:
                    emit_Ttrans(c + 1, i)
                    if i >= 1:
                        emit_Tsum(c + 1, i - 1)
                if dbg != "A1" and 1 <= i <= H:
                    emit_Kfin(c, i - 1)
            if nxt:
                emit_Tsum(c + 1, KC - 1)
                emit_hT(c + 1)

        if dbg and dbg[0] == "A":
            dbg_out = st["dbg_out"]
            k.barrier()
            dt_ = sb("dbgt", [128, 2048])
            bd = Buf("dbgt")
            k.op(DVE, lambda: nc.vector.memset(dt_[:], 0.0), wr=[bd])
            k.op(DVE, lambda: nc.vector.tensor_copy(out=dt_[:, 0:512], in_=hTc[:, 0, :]), rd=[b_hTc[0]], wr=[bd])
            if dbg == "A":
                k.op(DVE, lambda: nc.vector.tensor_copy(out=dt_[:, 512:640], in_=kmean[:].rearrange("p h n -> p (h n)")), rd=b_kmean, wr=[bd])
            k.op(DVE, lambda: nc.vector.tensor_copy(out=dt_[:, 1024:1536], in_=rs[:, 0, :]), rd=[b_rs[0]], wr=[bd])
            k.dma(SP, k.new_sem("dbg"), dbg_out[:, 0:2048], dt_[:], rd=[bd])
            k.barrier()
        k.barrier()
        k.arena.free(*local_names)
        if not early_release:
            ws.release(); ws.release(); ws.release()


def phase_rest(st):
    nc = st["nc"]; k = st["k"]; ws = st["ws"]; psum = st["psum"]; bank = st["bank"]
    PE, ACT, DVE, POOL, SP = k.PE, k.ACT, k.DVE, k.POOL, k.SP
    x_all = st["x_all"]; x_halo = st["x_halo"]; p_own = st["p_own"]; out = st["out"]
    kt_s = st["kt_s"]; v_s = st["v_s"]
    ident = st["ident"]; b_ident = st["b_ident"]; ident_bf = st["ident_bf"]; b_ident_bf = st["b_ident_bf"]
    ones_bf = st["ones_bf"]; b_ones = st["b_ones"]
    esel_in = st["esel_in"]; tri_in = st["tri_in"]
    gains_sb = st["gains_sb"]; b_gains = st["b_gains"]; gqk_sb = st["gqk_sb"]; b_gqk = st["b_gqk"]
    pastb_sb = st["pastb_sb"]; b_pastb = st["b_pastb"]; wconv_sb = st["wconv_sb"]; b_wconv = st["b_wconv"]
    kmean = st["kmean"]; b_kmean = st["b_kmean"]; kmean_bf = st["kmean_bf"]; b_kmean_bf = st["b_kmean_bf"]
    epsb = st["epsb"]; b_eps = st["b_eps"]
    h1T = st["h1T"]; b_h1T = st["b_h1T"]
    dbg = st["dbg"]; dbg_out = st["dbg_out"]
    SCALE = float(HD) ** -0.5

    cnt = {"sq": 0}

    def fm_norm(es_sq, es_rs, src_fn, src_bufs, ncols, nkc, inv_n, gain_fn, dst_fn, dst_bufs, nb_sum, tmp_rs, b_tmp_rs):
        sq, b_sq = es_sq
        for kc in range(nkc):
            si = cnt["sq"] % 3
            cnt["sq"] += 1
            k.op(ACT, lambda kc=kc, si=si: nc.scalar.activation(out=sq[:, si, 0:ncols], in_=src_fn(kc), func=AF.Square),
                 rd=[src_bufs[kc]], wr=[b_sq[si]])
            k.op(PE, lambda kc=kc, si=si: nc.tensor.matmul(psum[:, nb_sum, 0:ncols], lhsT=ones_bf[:], rhs=sq[:, si, 0:ncols],
                                                             start=(kc == 0), stop=(kc == nkc - 1)),
                 rd=[b_ones, b_sq[si]], wr=[bank[nb_sum]])
        k.op(ACT, lambda: nc.scalar.activation(out=tmp_rs[:, 0:ncols], in_=psum[:, nb_sum, 0:ncols], func=AF.Sqrt,
                                               bias=epsb[:], scale=inv_n),
             rd=[bank[nb_sum], b_eps], wr=[b_tmp_rs])
        k.op(DVE, lambda: nc.vector.reciprocal(out=tmp_rs[:, 0:ncols], in_=tmp_rs[:, 0:ncols]), rd=[b_tmp_rs], wr=[b_tmp_rs])
        for kc in range(nkc):
            k.op(DVE, lambda kc=kc: nc.vector.scalar_tensor_tensor(
                out=dst_fn(kc), in0=src_fn(kc), scalar=gain_fn(kc), in1=tmp_rs[:, 0:ncols], op0=ALU.mult, op1=ALU.mult),
                rd=[src_bufs[kc], b_gains, b_gqk, b_tmp_rs], wr=[dst_bufs[kc]])

    def mm_group(pb, ncols, pairs, rd, col0=0):
        def fn():
            n = len(pairs)
            for i, (l, r) in enumerate(pairs):
                inst = nc.tensor.matmul(psum[:, pb, col0:col0 + ncols], lhsT=l, rhs=r, start=(i == 0), stop=(i == n - 1))
            return inst
        k.op(PE, fn, rd=rd, wr=[bank[pb]])

    A = k.arena

    def sbB(name, shape, dt=F32, top=False):
        return A.alloc(name, list(shape), dt, top=top)

    if True:
        sq = sbB("sq2", [128, 3, 512], BF16, top=True);       b_sq = [Buf(f"sq2_{i}") for i in range(3)]
        rs = sbB("rs2", [128, 512], top=True);                b_rs = Buf("rs2")
        esel = sbB("esel", [128, 16 * 128], BF16, top=True);  b_esel = Buf("esel")
        tri = sbB("tri", [128, 2, 256], BF16, top=True);      b_tri = Buf("tri")
        k.dma(SP, k.new_sem("c"), esel[:], esel_in[:], wr=[b_esel])
        k.dma(SP, k.new_sem("c"), tri[:], tri_in[:], wr=[b_tri])

        QT = sbB("QT", [128, H, T], BF16);          b_QT = [Buf(f"QT{h}") for h in range(H)]
        attnT = sbB("attnT", [128, H, T], BF16);    b_attnT = [Buf(f"attnT{h}") for h in range(H)]
        hhT = sbB("hhT", [128, KC, 8], BF16);       b_hhT = [Buf(f"hhT{i}") for i in range(KC)]
        MBT = sbB("MBT", [128, H, T], BF16);        b_MBT = [Buf(f"MBT{h}") for h in range(H)]

        if True:
            xh = sbB("xh", [8, D], F32); b_xh = Buf("xh")
            xhT = sbB("xhT", [128, KC, 8], F32); b_xhT = [Buf(f"xhT{i}") for i in range(KC)]
            k.dma(SP, k.new_sem("xh"), xh[:], x_halo[:], wr=[b_xh])
            for kc in range(KC):
                k.op(PE, lambda kc=kc: nc.tensor.transpose(psum[:, 0, kc * 8:(kc + 1) * 8], xh[:, kc * 128:(kc + 1) * 128], ident[0:8, 0:8]),
                     rd=[b_xh, b_ident], wr=[bank[0]])
            k.op(DVE, lambda: nc.vector.tensor_copy(out=xhT[:].rearrange("p k t -> p (k t)"), in_=psum[:, 0, 0:128]),
                 rd=[bank[0]], wr=b_xhT)
            fm_norm((sq, b_sq), None, lambda kc: xhT[:, kc, :], b_xhT, 8, KC, 1.0 / D,
                    lambda kc: gains_sb[:, 0, kc:kc + 1], lambda kc: hhT[:, kc, :], b_hhT, 1, rs, b_rs)
            k.op(DVE, lambda: nc.vector.tensor_scalar(out=kmean_bf[:], in0=kmean[:], scalar1=1.0 / BLK, scalar2=None, op0=ALU.mult),
                 rd=b_kmean, wr=[b_kmean_bf])
            k.op(DVE, lambda: nc.vector.memset(MBT[:], 0.0), wr=b_MBT)
            k.barrier()
            A.free("xh", "xhT")

        rotq = [2, 3, 6, 7]
        qstate = {}

        def q_proj(idx, h, half, wb, wt, hh):
            pb = rotq[idx % 4]
            mm_group(pb, 512, [(wt[:, kc, hh * 128:(hh + 1) * 128], h1T[:, kc, half * 512:(half + 1) * 512]) for kc in range(KC)],
                     rd=[wb] + b_h1T)
            si = cnt["sq"] % 3
            cnt["sq"] += 1
            k.op(ACT, lambda: nc.scalar.activation(out=sq[:, si, :], in_=psum[:, pb, :], func=AF.Square), rd=[bank[pb]], wr=[b_sq[si]])
            qstate[idx] = (pb, si, h, half)

        def q_fin(idx):
            pb, si, h, half = qstate.pop(idx)
            k.op(PE, lambda: nc.tensor.matmul(psum[:, 4, :], lhsT=ones_bf[:], rhs=sq[:, si, :], start=True, stop=True),
                 rd=[b_ones, b_sq[si]], wr=[bank[4]])
            k.op(ACT, lambda: nc.scalar.activation(out=rs[:], in_=psum[:, 4, :], func=AF.Sqrt, bias=epsb[:], scale=1.0 / HD),
                 rd=[bank[4], b_eps], wr=[b_rs])
            k.op(DVE, lambda: nc.vector.reciprocal(out=rs[:], in_=rs[:]), rd=[b_rs], wr=[b_rs])
            k.op(DVE, lambda: nc.vector.scalar_tensor_tensor(
                out=QT[:, h, half * 512:(half + 1) * 512], in0=psum[:, pb, :], scalar=gqk_sb[:, 0:1], in1=rs[:],
                op0=ALU.mult, op1=ALU.mult),
                rd=[bank[pb], b_gqk, b_rs], wr=[b_QT[h]])

        idx = 0
        for qh in range(2):
            wb, wt = ws.acquire(f"wq{qh}")
            for hh in range(4):
                for half in range(2):
                    q_proj(idx, qh * 4 + hh, half, wb, wt, hh)
                    if idx >= 1:
                        q_fin(idx - 1)
                    idx += 1
            ws.release()
        q_fin(idx - 1)

        if True:
            gb = sbB("gb", [128, H, 8, 16], F32); b_gb = [Buf(f"gb{h}") for h in range(H)]
            m8 = sbB("m8", [128, H, 8, 8], F32); b_m8 = [Buf(f"m8{h}") for h in range(H)]
            mb = sbB("mb", [128, H, 8, 16], F32); b_mb = [Buf(f"mb{h}") for h in range(H)]
            for h in range(H):
                pbg = 5 + h // 4
                c0 = (h % 4) * 128

                def gate_mm(h=h, pbg=pbg, c0=c0):
                    for tt in range(8):
                        inst = nc.tensor.matmul(psum[:, pbg, c0 + tt * 16:c0 + (tt + 1) * 16], lhsT=QT[:, h, tt * 128:(tt + 1) * 128],
                                                rhs=kmean_bf[:, h, :], start=True, stop=True)
                    return inst
                k.op(PE, gate_mm, rd=[b_QT[h], b_kmean_bf], wr=[bank[pbg]])
            for h in range(H):
                pbg = 5 + h // 4
                c0 = (h % 4) * 128
                k.op(DVE, lambda h=h, pbg=pbg, c0=c0: nc.vector.tensor_tensor(out=gb[:, h].rearrange("p a b -> p (a b)"), in0=psum[:, pbg, c0:c0 + 128],
                                                                              in1=pastb_sb[:, 0].rearrange("p a b -> p (a b)"), op=ALU.add),
                     rd=[bank[pbg], b_pastb], wr=[b_gb[h]])
                for tt in range(8):
                    k.op(DVE, lambda h=h, tt=tt: nc.vector.max(out=m8[:, h, tt, :], in_=gb[:, h, tt, :]), rd=[b_gb[h]], wr=[b_m8[h]])
                for tt in range(8):
                    k.op(DVE, lambda h=h, tt=tt: nc.vector.tensor_scalar(out=mb[:, h, tt, :], in0=gb[:, h, tt, :], scalar1=m8[:, h, tt, 2:3], scalar2=-NEG,
                                                                         op0=ALU.is_ge, op1=ALU.mult),
                         rd=[b_gb[h], b_m8[h]], wr=[b_mb[h]])
                k.op(DVE, lambda h=h: nc.vector.scalar_tensor_tensor(out=mb[:, h].rearrange("p a b -> p (a b)"), in0=mb[:, h].rearrange("p a b -> p (a b)"),
                                                                     scalar=NEG, in1=pastb_sb[:, 1].rearrange("p a b -> p (a b)"),
                                                                     op0=ALU.add, op1=ALU.add),
                     rd=[b_mb[h], b_pastb], wr=[b_mb[h]])
                pb0 = 0 if h % 2 == 0 else 2
                for tt in range(8):
                    pbt = pb0 + (tt // 4)
                    k.op(PE, lambda h=h, tt=tt, pbt=pbt: nc.tensor.transpose(psum[0:16, pbt, (tt % 4) * 128:(tt % 4 + 1) * 128], mb[:, h, tt, :], ident[:]),
                         rd=[b_mb[h], b_ident], wr=[bank[pbt]])
                for hf in range(2):
                    k.op(ACT, lambda h=h, hf=hf, pb0=pb0: nc.scalar.copy(out=MBT[0:16, h, hf * 512:(hf + 1) * 512], in_=psum[0:16, pb0 + hf, :]),
                         rd=[bank[pb0 + hf]], wr=[b_MBT[h]])
            k.barrier()
            A.free("gb", "m8", "mb")

        if True:
            KTh = sbB("KTh", [128, 2, S], BF16); b_KTh = [Buf("KTh0"), Buf("KTh1")]
            Vh = sbB("Vh", [128, 2, 32, HD], BF16); b_Vh = [Buf("Vh0"), Buf("Vh1")]
            s_K = [k.new_sem("KTh0"), k.new_sem("KTh1")]
            s_V = [k.new_sem("Vh0"), k.new_sem("Vh1")]
            NPT = 6
            PT = sbB("PT", [128, NPT, 512], BF16); b_PT = [Buf(f"PT{i}") for i in range(NPT)]
            rden = sbB("rden", [128, 256], F32); b_rden = Buf("rden")
            accD = sbB("accD", [128, 2, 512], F32); b_accD = [Buf("accD0"), Buf("accD1")]
            accP = sbB("accP", [128, 2, 512], F32); b_accP = [Buf("accP0"), Buf("accP1")]
            accb = sbB("accb", [128, 2, 512], BF16); b_accb = [Buf("accb0"), Buf("accb1")]
            slot_ctr = [0]

            def load_kv(h):
                i = h % 2
                k.dma(SP, s_K[i], KTh[:, i, :], kt_s[h, :, :], wr=[b_KTh[i]])
                k.dma(SP, s_V[i], Vh[:, i, :, :], v_s[:, h * HD:(h + 1) * HD].rearrange("(t p) d -> p t d", p=128), wr=[b_Vh[i]])

            load_kv(0)
            pti = 0
            for h in range(H):
                if h + 1 < H:
                    load_kv(h + 1)
                hi = h % 2
                for s_ in range(4):
                    nblk = 4 * s_ + 4
                    q_ap = QT[:, h, s_ * 256:(s_ + 1) * 256]
                    pb_o = 3 + (s_ % 2)
                    pb_d = 5 + (s_ % 2)

                    def emit_qk(n, pbs):
                        def fn():
                            for i in range(2):
                                kt = 2 * n + i
                                nc.tensor.matmul(psum[:, pbs, i * 256:(i + 1) * 256], lhsT=KTh[:, hi, kt * 128:(kt + 1) * 128], rhs=q_ap,
                                                 start=True, stop=False)
                                if n < nblk - 1:
                                    inst = nc.tensor.matmul(psum[:, pbs, i * 256:(i + 1) * 256], lhsT=esel[:, n * 128:(n + 1) * 128],
                                                            rhs=MBT[:, h, s_ * 256:(s_ + 1) * 256], start=False, stop=True)
                                else:
                                    inst = nc.tensor.matmul(psum[:, pbs, i * 256:(i + 1) * 256], lhsT=ident_bf[:], rhs=tri[:, i, :],
                                                            start=False, stop=True)
                            return inst
                        k.op(PE, fn, rd=[b_KTh[hi], b_QT[h], b_esel, b_MBT[h], b_ident_bf, b_tri], wr=[bank[pbs]])

                    def emit_exp(n, pbs, pi):
                        k.op(ACT, lambda: nc.scalar.activation(out=PT[:, pi, :], in_=psum[:, pbs, :], func=AF.Exp, scale=SCALE),
                             rd=[bank[pbs]], wr=[b_PT[pi]])

                    sp = slot_ctr[0] % 2
                    slot_ctr[0] += 1
                    first = {"D": True, "P": True}

                    def emit_pv(n, pi):
                        def fn():
                            for i in range(2):
                                kt = 2 * n + i
                                inst = nc.tensor.matmul(psum[:, pb_o, 0:256], lhsT=Vh[:, hi, kt, :], rhs=PT[:, pi, i * 256:(i + 1) * 256],
                                                        start=(n == 0 and i == 0), stop=(n == nblk - 1 and i == 1))
                            return inst
                        k.op(PE, fn, rd=[b_Vh[hi], b_PT[pi]], wr=[bank[pb_o]])
                        if n % 3 == 2:
                            if first["P"]:
                                k.op(POOL, lambda: nc.gpsimd.tensor_copy(out=accP[:, sp, :], in_=PT[:, pi, :]), rd=[b_PT[pi]], wr=[b_accP[sp]])
                                first["P"] = False
                            else:
                                k.op(POOL, lambda: nc.gpsimd.tensor_tensor(out=accP[:, sp, :], in0=accP[:, sp, :], in1=PT[:, pi, :], op=ALU.add),
                                     rd=[b_PT[pi], b_accP[sp]], wr=[b_accP[sp]])
                        else:
                            if first["D"]:
                                k.op(DVE, lambda: nc.vector.tensor_copy(out=accD[:, sp, :], in_=PT[:, pi, :]), rd=[b_PT[pi]], wr=[b_accD[sp]])
                                first["D"] = False
                            else:
                                k.op(DVE, lambda: nc.vector.tensor_tensor(out=accD[:, sp, :], in0=accD[:, sp, :], in1=PT[:, pi, :], op=ALU.add),
                                     rd=[b_PT[pi], b_accD[sp]], wr=[b_accD[sp]])

                    LOOK = 2
                    slots = {}
                    for n in range(min(LOOK, nblk)):
                        pbs = n % 3
                        emit_qk(n, pbs)
                    for n in range(nblk):
                        pbs = n % 3
                        pi = pti % NPT
                        pti += 1
                        emit_exp(n, pbs, pi)
                        if n + LOOK < nblk:
                            emit_qk(n + LOOK, (n + LOOK) % 3)
                        emit_pv(n, pi)
                    k.op(DVE, lambda: nc.vector.tensor_tensor(out=accb[:, sp, :], in0=accD[:, sp, :], in1=accP[:, sp, :], op=ALU.add),
                         rd=[b_accD[sp], b_accP[sp]], wr=[b_accb[sp]])

                    def den_mm():
                        nc.tensor.matmul(psum[:, pb_d, 0:256], lhsT=ones_bf[:], rhs=accb[:, sp, 0:256], start=True, stop=False)
                        return nc.tensor.matmul(psum[:, pb_d, 0:256], lhsT=ones_bf[:], rhs=accb[:, sp, 256:512], start=False, stop=True)
                    k.op(PE, den_mm, rd=[b_ones, b_accb[sp]], wr=[bank[pb_d]])
                    k.op(DVE, lambda pb_d=pb_d: nc.vector.reciprocal(out=rden[:], in_=psum[:, pb_d, 0:256]), rd=[bank[pb_d]], wr=[b_rden])
                    k.op(DVE, lambda pb_o=pb_o, h=h, s_=s_: nc.vector.tensor_tensor(out=attnT[:, h, s_ * 256:(s_ + 1) * 256], in0=psum[:, pb_o, 0:256],
                                                                                     in1=rden[:], op=ALU.mult),
                         rd=[bank[pb_o], b_rden], wr=[b_attnT[h]])
            k.barrier()
            A.free("KTh", "Vh", "PT", "rden", "accD", "accP", "accb", "QT", "MBT", "esel", "tri")

        if dbg == "B3":
            if True:
                dt_ = sbB("dbgt", [128, 4096], F32); bd = Buf("dbgt")
                k.op(DVE, lambda: nc.vector.memset(dt_[:], 0.0), wr=[bd])
                k.op(DVE, lambda: nc.vector.tensor_copy(out=dt_[:, 0:1024], in_=QT[:, 0, :]), rd=b_QT, wr=[bd])
                k.op(DVE, lambda: nc.vector.tensor_copy(out=dt_[:, 1024:2048], in_=attnT[:, 0, :]), rd=b_attnT, wr=[bd])
                k.op(DVE, lambda: nc.vector.tensor_copy(out=dt_[:, 2048:3072], in_=MBT[:, 0, :]), rd=b_MBT, wr=[bd])
                k.op(DVE, lambda: nc.vector.tensor_copy(out=dt_[:, 3072:4096], in_=attnT[:, 7, :]), rd=b_attnT, wr=[bd])
                k.dma(SP, k.new_sem("dbg"), dbg_out[:, 0:4096], dt_[:], rd=[bd])
                k.barrier()
                return

        mergedT = sbB("mergedT", [128, KC, T], BF16, top=True); b_mg = [Buf(f"mg{i}") for i in range(KC)]
        vT = sbB("vT", [128, 8, T], BF16);          b_vT = [Buf(f"vT{i}") for i in range(8)]
        if True:
            ccs = sbB("ccs", [128, 4, T + 8], F32); b_ccs = [Buf(f"ccs{i}") for i in range(4)]
            upad = sbB("upad", [128, 4, 4, 258], F32); b_upad = [Buf(f"upad{i}") for i in range(4)]
            for hf in range(2):
                wb, wt = ws.acquire(f"wcc{hf}")
                for ch in range(4):
                    for half in range(2):
                        pb = (ch * 2 + half) % 2
                        mm_group(pb, 512, [(wt[:, kc, ch * 128:(ch + 1) * 128], h1T[:, kc, half * 512:(half + 1) * 512]) for kc in range(KC)],
                                 rd=[wb] + b_h1T)
                        k.op(ACT, lambda ch=ch, half=half, pb=pb: nc.scalar.copy(out=ccs[:, ch, half * 512:(half + 1) * 512], in_=psum[:, pb, :]),
                             rd=[bank[pb]], wr=[b_ccs[ch]])
                    mm_group(2, 8, [(wt[:, kc, ch * 128:(ch + 1) * 128], hhT[:, kc, :]) for kc in range(KC)], rd=[wb] + b_hhT)
                    k.op(ACT, lambda ch=ch: nc.scalar.copy(out=ccs[:, ch, T:T + 8], in_=psum[:, 2, 0:8]), rd=[bank[2]], wr=[b_ccs[ch]])
                ws.release()
                wb, wt = ws.acquire(f"wcx{hf}")
                for ch in range(4):
                    for half in range(2):
                        pb = (ch * 2 + half) % 2
                        mm_group(pb, 512, [(wt[:, kc, ch * 128:(ch + 1) * 128], h1T[:, kc, half * 512:(half + 1) * 512]) for kc in range(KC)],
                                 rd=[wb] + b_h1T)
                        k.op(DVE, lambda ch=ch, half=half, pb=pb: nc.vector.tensor_tensor(
                            out=upad[:, ch, 2 * half:2 * half + 2, 2:258], in0=psum[:, pb, :].rearrange("p (b t) -> p b t", b=2),
                            in1=ccs[:, ch, half * 512:(half + 1) * 512].rearrange("p (b t) -> p b t", b=2), op=ALU.mult),
                            rd=[bank[pb], b_ccs[ch]], wr=[b_upad[ch]])
                    mm_group(2, 8, [(wt[:, kc, ch * 128:(ch + 1) * 128], hhT[:, kc, :]) for kc in range(KC)], rd=[wb] + b_hhT)
                    k.op(DVE, lambda ch=ch: nc.vector.tensor_tensor(
                        out=upad[:, ch, :, 0:2], in0=psum[:, 2, 0:8].rearrange("p (b t) -> p b t", b=4),
                        in1=ccs[:, ch, T:T + 8].rearrange("p (b t) -> p b t", b=4), op=ALU.mult),
                        rd=[bank[2], b_ccs[ch]], wr=[b_upad[ch]])
                    gch = hf * 4 + ch
                    acc = ccs[:, ch, 0:T].rearrange("p (b t) -> p b t", b=4)
                    k.op(DVE, lambda ch=ch, gch=gch, acc=acc: nc.vector.tensor_scalar(out=acc, in0=upad[:, ch, :, 2:258], scalar1=wconv_sb[:, gch, 2:3],
                                                                                    scalar2=None, op0=ALU.mult),
                         rd=[b_upad[ch], b_wconv], wr=[b_ccs[ch]])
                    for tap in (1, 0):
                        k.op(DVE, lambda ch=ch, gch=gch, acc=acc, tap=tap: nc.vector.scalar_tensor_tensor(
                            out=acc, in0=upad[:, ch, :, tap:tap + 256], scalar=wconv_sb[:, gch, tap:tap + 1], in1=acc,
                            op0=ALU.mult, op1=ALU.add),
                            rd=[b_upad[ch], b_wconv, b_ccs[ch]], wr=[b_ccs[ch]])
                ws.release()
                wb, wt = ws.acquire(f"wcb{hf}")
                for ch in range(4):
                    gch = hf * 4 + ch
                    for half in range(2):
                        pb = (ch * 2 + half) % 2
                        mm_group(pb, 512, [(wt[:, kc, ch * 128:(ch + 1) * 128], h1T[:, kc, half * 512:(half + 1) * 512]) for kc in range(KC)],
                                 rd=[wb] + b_h1T)
                        k.op(DVE, lambda ch=ch, gch=gch, half=half, pb=pb: nc.vector.tensor_tensor(
                            out=vT[:, gch, half * 512:(half + 1) * 512], in0=psum[:, pb, :], in1=ccs[:, ch, half * 512:(half + 1) * 512], op=ALU.mult),
                            rd=[bank[pb], b_ccs[ch]], wr=[b_vT[gch]])
                ws.release()
            k.barrier()
            A.free("ccs", "upad", "hhT")

        if True:
            sg = sbB("sg", [128, 2, 4, T], BF16); b_sg = [[Buf(f"sg{a}_{i}") for i in range(4)] for a in range(2)]
            m1 = sbB("m1", [128, 4, T], F32); b_m1 = [Buf(f"m1_{i}") for i in range(4)]
            for cb in range(4):
                for which, (gname, wname, srcT, b_src, nk) in enumerate(((f"wgc{cb}", f"wco{cb}", vT, b_vT, 8), (f"wga{cb}", f"wao{cb}", attnT, b_attnT, 8))):
                    wb, wt = ws.acquire(gname)
                    for ch in range(4):
                        for half in range(2):
                            pb = (ch * 2 + half) % 2
                            mm_group(pb, 512, [(wt[:, kc, ch * 128:(ch + 1) * 128], h1T[:, kc, half * 512:(half + 1) * 512]) for kc in range(KC)],
                                     rd=[wb] + b_h1T)
                            k.op(ACT, lambda which=which, ch=ch, half=half, pb=pb: nc.scalar.activation(
                                out=sg[:, which, ch, half * 512:(half + 1) * 512], in_=psum[:, pb, :], func=AF.Sigmoid),
                                rd=[bank[pb]], wr=[b_sg[which][ch]])
                    ws.release()
                    wb, wt = ws.acquire(wname)
                    for ch in range(4):
                        gch = cb * 4 + ch
                        for half in range(2):
                            pb = 2 + (ch * 2 + half) % 2
                            mm_group(pb, 512, [(wt[:, kc, ch * 128:(ch + 1) * 128], srcT[:, kc, half * 512:(half + 1) * 512]) for kc in range(nk)],
                                     rd=[wb] + b_src)
                            if which == 0:
                                k.op(DVE, lambda ch=ch, half=half, pb=pb: nc.vector.tensor_tensor(
                                    out=m1[:, ch, half * 512:(half + 1) * 512], in0=psum[:, pb, :], in1=sg[:, 0, ch, half * 512:(half + 1) * 512], op=ALU.mult),
                                    rd=[bank[pb], b_sg[0][ch]], wr=[b_m1[ch]])
                            else:
                                k.op(DVE, lambda ch=ch, half=half, pb=pb: nc.vector.tensor_tensor(
                                    out=sg[:, 1, ch, half * 512:(half + 1) * 512], in0=psum[:, pb, :], in1=sg[:, 1, ch, half * 512:(half + 1) * 512], op=ALU.mult),
                                    rd=[bank[pb], b_sg[1][ch]], wr=[b_sg[1][ch]])
                                k.op(DVE, lambda ch=ch, gch=gch, half=half: nc.vector.tensor_tensor(
                                    out=mergedT[:, gch, half * 512:(half + 1) * 512], in0=sg[:, 1, ch, half * 512:(half + 1) * 512],
                                    in1=m1[:, ch, half * 512:(half + 1) * 512], op=ALU.add),
                                    rd=[b_sg[1][ch], b_m1[ch]], wr=[b_mg[gch]])
                    ws.release()
            k.barrier()
            A.free("sg", "m1", "vT", "attnT", "h1T")

        rT = sbB("rT", [128, KC, T], F32); b_rT = [Buf(f"rT{i}") for i in range(KC)]
        xt = sbB("xt2", [128, 2, D], F32); b_xt = [Buf("xt2_0"), Buf("xt2_1")]
        s_xt = [k.new_sem("xt2_0"), k.new_sem("xt2_1")]
        for tile in range(8):
            so, tl = divmod(tile, 2)
            row0 = (4 * so + 3) * BLK + tl * 128
            xi = tile % 2
            k.dma(SP, s_xt[xi], xt[:, xi, :], x_all[row0:row0 + 128, :], wr=[b_xt[xi]])
            for g4 in range(4):
                pb = (tile * 4 + g4) % 2
                for q in range(4):
                    kc = g4 * 4 + q
                    k.op(PE, lambda xi=xi, kc=kc, q=q, pb=pb: nc.tensor.transpose(psum[:, pb, q * 128:(q + 1) * 128], xt[:, xi, kc * 128:(kc + 1) * 128], ident[:]),
                         rd=[b_xt[xi], b_ident], wr=[bank[pb]])
                k.op(ACT, lambda g4=g4, tile=tile, pb=pb: nc.scalar.copy(out=rT[:, g4 * 4:g4 * 4 + 4, tile * 128:(tile + 1) * 128],
                                                                         in_=psum[:, pb, :].rearrange("p (q t) -> p q t", q=4)),
                     rd=[bank[pb]], wr=b_rT[g4 * 4:g4 * 4 + 4])
        for cb in range(4):
            wb, wt = ws.acquire(f"wo{cb}")
            for ch in range(4):
                gch = cb * 4 + ch
                for half in range(2):
                    pb = 2 + (ch * 2 + half) % 2
                    mm_group(pb, 512, [(wt[:, kc, ch * 128:(ch + 1) * 128], mergedT[:, kc, half * 512:(half + 1) * 512]) for kc in range(KC)],
                             rd=[wb] + b_mg)
                    k.op(DVE, lambda gch=gch, half=half, pb=pb: nc.vector.tensor_tensor(
                        out=rT[:, gch, half * 512:(half + 1) * 512], in0=psum[:, pb, :], in1=rT[:, gch, half * 512:(half + 1) * 512], op=ALU.add),
                        rd=[bank[pb], b_rT[gch]], wr=[b_rT[gch]])
            ws.release()
        k.barrier()
        A.free("mergedT", "xt2")

        if dbg == "B":
            dt_ = sbB("dbgt", [128, 2048], F32); bd = Buf("dbgt")
            k.op(DVE, lambda: nc.vector.tensor_copy(out=dt_[:, 0:1024], in_=rT[:, 0, :]), rd=b_rT, wr=[bd])
            k.op(DVE, lambda: nc.vector.tensor_copy(out=dt_[:, 1024:2048], in_=rT[:, 5, :]), rd=b_rT, wr=[bd])
            k.dma(SP, k.new_sem("dbg"), dbg_out[:, 0:2048], dt_[:], rd=[bd])
            k.barrier()
            return

        hT = sbB("hT", [128, KC, T], BF16); b_hT = [Buf(f"hT{i}") for i in range(KC)]
        hid = sbB("hid", [128, 2, 8, T], BF16); b_hid = [[Buf(f"hid{a_}_{i}") for i in range(8)] for a_ in range(2)]
        tsq = sbB("tsq", [128, 2, 512], F32); b_tsq = [Buf("tsq0"), Buf("tsq1")]
        for half in range(2):
            hs = slice(half * 512, (half + 1) * 512)
            fm_norm((sq, b_sq), None, lambda kc: rT[:, kc, hs], b_rT, 512, KC, 1.0 / D,
                    lambda kc: gains_sb[:, 1, kc:kc + 1], lambda kc: hT[:, kc, hs], b_hT, 7, rs, b_rs)
        ti = 0
        ui = 0
        di = 0
        for j in range(8):
            hb = j % 2
            for cbh in range(2):
                wb, wt = ws.acquire(f"wup{j}_{cbh}")
                for ch in range(4):
                    lch = cbh * 4 + ch
                    for half in range(2):
                        hs = slice(half * 512, (half + 1) * 512)
                        pb = ui % 4
                        ui += 1
                        mm_group(pb, 512, [(wt[:, kc, ch * 128:(ch + 1) * 128], hT[:, kc, hs]) for kc in range(KC)], rd=[wb] + b_hT)
                        tq = ti % 2
                        ti += 1
                        k.op(ACT, lambda tq=tq, pb=pb: nc.scalar.activation(out=tsq[:, tq, :], in_=psum[:, pb, :], func=AF.Square),
                             rd=[bank[pb]], wr=[b_tsq[tq]])
                        k.op(DVE, lambda tq=tq, pb=pb, lch=lch, hb=hb, hs=hs: nc.vector.scalar_tensor_tensor(
                            out=hid[:, hb, lch, hs], in0=psum[:, pb, :], scalar=0.0, in1=tsq[:, tq, :], op0=ALU.is_gt, op1=ALU.mult),
                            rd=[bank[pb], b_tsq[tq]], wr=[b_hid[hb][lch]])
                ws.release()
            for cbh in range(2):
                wb, wt = ws.acquire(f"wdn{j}_{cbh}")
                for cc in range(8):
                    gch = cbh * 8 + cc
                    for half in range(2):
                        hs = slice(half * 512, (half + 1) * 512)
                        pb = 4 + di % 3
                        di += 1
                        mm_group(pb, 512, [(wt[:, kc, cc * 128:(cc + 1) * 128], hid[:, hb, kc, hs]) for kc in range(8)], rd=[wb] + b_hid[hb])
                        k.op(DVE, lambda gch=gch, pb=pb, hs=hs: nc.vector.tensor_tensor(out=rT[:, gch, hs], in0=psum[:, pb, :], in1=rT[:, gch, hs], op=ALU.add),
                             rd=[bank[pb], b_rT[gch]], wr=[b_rT[gch]])
                ws.release()
        k.barrier()
        A.free("hid", "tsq")
        if dbg == "C":
            dt_ = sbB("dbgt", [128, 2048], F32); bd = Buf("dbgt")
            k.op(DVE, lambda: nc.vector.tensor_copy(out=dt_[:, 0:1024], in_=rT[:, 0, :]), rd=b_rT, wr=[bd])
            k.op(DVE, lambda: nc.vector.tensor_copy(out=dt_[:, 1024:2048], in_=rT[:, 5, :]), rd=b_rT, wr=[bd])
            k.dma(SP, k.new_sem("dbg"), dbg_out[:, 0:2048], dt_[:], rd=[bd])
            k.barrier()
            A.free("dbgt")

        A.free("hT")
        h3T = sbB("h3T", [128, KC, T], BF16); b_h3T = [Buf(f"h3T{i}") for i in range(KC)]
        pT = sbB("pT", [128, 2, T], BF16); b_pT = [Buf("pT0"), Buf("pT1")]
        pt = sbB("pt", [128, 8, PLE], F32); b_pt = Buf("pt")
        sgp = sbB("sgp", [128, 2, 512], F32); b_sgp = [Buf("sgp0"), Buf("sgp1")]
        tmp = sbB("tmpp", [128, 2, 512], F32); b_tmp = [Buf("tmpp0"), Buf("tmpp1")]
        k.dma(SP, k.new_sem("pt"), pt[:], p_own.rearrange("(t p) c -> p t c", p=128), wr=[b_pt])
        for tile in range(8):
            pb = tile % 2
            for c2 in range(2):
                k.op(PE, lambda tile=tile, c2=c2, pb=pb: nc.tensor.transpose(psum[:, pb, c2 * 128:(c2 + 1) * 128], pt[:, tile, c2 * 128:(c2 + 1) * 128], ident[:]),
                     rd=[b_pt, b_ident], wr=[bank[pb]])
            k.op(ACT, lambda tile=tile, pb=pb: nc.scalar.copy(out=pT[:, :, tile * 128:(tile + 1) * 128],
                                                              in_=psum[:, pb, 0:256].rearrange("p (c t) -> p c t", c=2)),
                 rd=[bank[pb]], wr=b_pT)
        for half in range(2):
            hs = slice(half * 512, (half + 1) * 512)
            fm_norm((sq, b_sq), None, lambda kc: rT[:, kc, hs], b_rT, 512, KC, 1.0 / D,
                    lambda kc: gains_sb[:, 2, kc:kc + 1], lambda kc: h3T[:, kc, hs], b_h3T, 7, rs, b_rs)
        wtp = sbB("wppb", [128, 2, D], BF16); wbp = Buf("wppb")
        k.dma(POOL, k.new_sem("wppb"), wtp[:], st["w_pp"].rearrange("(k p) c -> p k c", p=128), wr=[wbp])
        gi = 0
        for cb in range(4):
            wb, wt = ws.acquire(f"wpg{cb}")
            for ch in range(4):
                gch = cb * 4 + ch
                for half in range(2):
                    hs = slice(half * 512, (half + 1) * 512)
                    pb = gi % 2
                    pb2 = 2 + gi % 2
                    g2 = gi % 2
                    gi += 1
                    mm_group(pb, 512, [(wt[:, kc, ch * 128:(ch + 1) * 128], h3T[:, kc, hs]) for kc in range(KC)], rd=[wb] + b_h3T)
                    k.op(ACT, lambda g2=g2, pb=pb: nc.scalar.activation(out=sgp[:, g2, :], in_=psum[:, pb, :], func=AF.Sigmoid),
                         rd=[bank[pb]], wr=[b_sgp[g2]])
                    mm_group(pb2, 512, [(wtp[:, kc, gch * 128:(gch + 1) * 128], pT[:, kc, hs]) for kc in range(2)], rd=[wbp] + b_pT)
                    k.op(DVE, lambda g2=g2, pb2=pb2: nc.vector.tensor_tensor(out=tmp[:, g2, :], in0=psum[:, pb2, :], in1=sgp[:, g2, :], op=ALU.mult),
                         rd=[bank[pb2], b_sgp[g2]], wr=[b_tmp[g2]])
                    k.op(DVE, lambda g2=g2, gch=gch, hs=hs: nc.vector.tensor_tensor(out=rT[:, gch, hs], in0=tmp[:, g2, :], in1=rT[:, gch, hs], op=ALU.add),
                         rd=[b_tmp[g2], b_rT[gch]], wr=[b_rT[gch]])
            ws.release()
        k.barrier()
        A.free("h3T", "pT", "pt", "sgp", "tmpp", "wppb")
        if dbg == "C":
            dt_ = sbB("dbgt", [128, 2048], F32); bd = Buf("dbgt")
            k.op(DVE, lambda: nc.vector.tensor_copy(out=dt_[:, 0:1024], in_=rT[:, 0, :]), rd=b_rT, wr=[bd])
            k.op(DVE, lambda: nc.vector.tensor_copy(out=dt_[:, 1024:2048], in_=rT[:, 5, :]), rd=b_rT, wr=[bd])
            k.dma(SP, k.new_sem("dbg"), dbg_out[:, 2048:4096], dt_[:], rd=[bd])
            k.barrier()
            A.free("dbgt")

        ot = sbB("ot", [128, 2, D], F32); b_ot = [Buf("ot0"), Buf("ot1")]
        s_ot = [k.new_sem("ot0"), k.new_sem("ot1")]
        for tile in range(8):
            oi = tile % 2
            for g4 in range(4):
                pb = (tile * 4 + g4) % 2
                for q in range(4):
                    kc = g4 * 4 + q
                    k.op(PE, lambda kc=kc, q=q, pb=pb, tile=tile: nc.tensor.transpose(psum[:, pb, q * 128:(q + 1) * 128], rT[:, kc, tile * 128:(tile + 1) * 128], ident[:]),
                         rd=[b_rT[kc], b_ident], wr=[bank[pb]])
                eng = ACT if g4 % 2 == 0 else DVE
                if eng is ACT:
                    k.op(ACT, lambda oi=oi, g4=g4, pb=pb: nc.scalar.copy(out=ot[:, oi, g4 * 512:(g4 + 1) * 512], in_=psum[:, pb, :]), rd=[bank[pb]], wr=[b_ot[oi]])
                else:
                    k.op(DVE, lambda oi=oi, g4=g4, pb=pb: nc.vector.tensor_copy(out=ot[:, oi, g4 * 512:(g4 + 1) * 512], in_=psum[:, pb, :]), rd=[bank[pb]], wr=[b_ot[oi]])
            k.dma(SP, s_ot[oi], out[tile * 128:(tile + 1) * 128, :], ot[:, oi, :], rd=[b_ot[oi]])
        k.barrier()


def make_inputs(x, p, g_mix, w_in, g_q, g_k, w_conv, w_attn_out, w_conv_out, w_o,
                g_mlp, w_up, w_down, g_ple, w_ple_gate, w_ple_proj):
    f = np.float32
    shared = {
        "w_in": np.ascontiguousarray(w_in[0], f),
        "w_conv": np.ascontiguousarray(np.asarray(w_conv[0], f).T.reshape(8, 128, 3).transpose(1, 0, 2)),
        "w_ao": np.ascontiguousarray(w_attn_out[0], f),
        "w_co": np.ascontiguousarray(w_conv_out[0], f),
        "w_o": np.ascontiguousarray(w_o[0], f),
        "w_up": np.ascontiguousarray(w_up[0], f),
        "w_down": np.ascontiguousarray(w_down[0], f),
        "w_pg": np.ascontiguousarray(w_ple_gate[0], f),
        "w_pp": np.ascontiguousarray(w_ple_proj[0], f),
        "gains": np.ascontiguousarray(np.stack([np.asarray(g, f)[0].reshape(KC, 128).T for g in (g_mix, g_mlp, g_ple)], axis=1)),
        "gqk": np.ascontiguousarray(np.stack([np.asarray(g_q, f)[0], np.asarray(g_k, f)[0]], axis=1)),
        "ident": np.eye(128, dtype=f),
    }
    esel = np.zeros((128, 16 * 128), f)
    for n in range(16):
        esel[n, n * 128:(n + 1) * 128] = 1.0
    shared["esel"] = esel.astype(ml_dtypes.bfloat16)
    tri = np.zeros((128, 2, 256), f)
    for kt in range(2):
        kk = kt * 128 + np.arange(128)[:, None]
        qq = np.arange(256)[None, :]
        tri[:, kt, :] = np.where(kk <= qq, 0.0, NEG)
    shared["tri"] = tri.astype(ml_dtypes.bfloat16)
    x = np.asarray(x, f)
    p = np.asarray(p, f)
    in_maps = []
    metas = []
    for r in range(8):
        b, j = divmod(r, 4)
        own, perm = core_perm(j)
        xb = x[b].reshape(NBLK, BLK, D)
        m = dict(shared)
        m["x_all"] = np.ascontiguousarray(xb[perm].reshape(S, D))
        halo = np.zeros((8, D), f)
        for s in range(4):
            g = own[s]
            if g > 0:
                halo[2 * s:2 * s + 2] = xb[g - 1, BLK - 2:BLK]
        m["x_halo"] = halo
        m["p_own"] = np.ascontiguousarray(p[0, b].reshape(NBLK, BLK, PLE)[own].reshape(T, PLE))
        pb = np.zeros((2, 8, 16), f)
        pb[0] = -BIG
        pb[1] = NEG
        for tt in range(8):
            s = tt // 2
            for pos in range(4 * s + 3):
                if perm[pos] < own[s]:
                    pb[0, tt, pos] = 0.0
                    pb[1, tt, pos] = 0.0
        m["pastb"] = np.ascontiguousarray(np.broadcast_to(pb[None], (128, 2, 8, 16)))
        in_maps.append(m)
        metas.append((b, own))
    return in_maps, metas


_NC_CACHE = {}


def kernel(**inputs):
    in_maps, metas = make_inputs(**inputs)
    if "nc" not in _NC_CACHE:
        _NC_CACHE["nc"] = build_program(DEBUG)
    nc = _NC_CACHE["nc"]
    in_maps = [{n: m[n] for n in nc._mk_declared} for m in in_maps]
    res = run_bass_kernel_spmd(nc, in_maps, core_ids=list(range(8)))
    outp = np.zeros((2, S, D), np.float32)
    for r in range(8):
        b, own = metas[r]
        o = np.asarray(res.results[r]["out"], np.float32).reshape(4, BLK, D)
        for s in range(4):
            outp[b, own[s] * BLK:(own[s] + 1) * BLK] = o[s]
    if DEBUG:
        kernel.dbg = [np.asarray(res.results[r]["dbg"]) for r in range(8)]
    return outp
```

```python
import os
from contextlib import ExitStack

import numpy as np
import ml_dtypes

import concourse.bass as bass
import concourse.mybir as mybir
from concourse.bass_utils import run_bass_kernel_spmd

F32 = mybir.dt.float32
BF16 = mybir.dt.bfloat16
AF = mybir.ActivationFunctionType
ALU = mybir.AluOpType
AX = mybir.AxisListType

D = 2048
KC = D // 128
S = 4096
NBLK = 16
BLK = 256
H = 8
HD = 128
DFF = 8192
PLE = 256
EPS = 1e-6
T = 1024
NEG = -30000.0
BIG = 1.0e30

DEBUG = os.environ.get("MK_DEBUG", "")


class Sem:
    def __init__(self, h, name):
        self.h = h
        self.v = 0
        self.name = name


class Buf:
    __slots__ = ("name", "w", "rd", "excl")

    def __init__(self, name, excl=False):
        self.name = name
        self.w = None
        self.rd = {}
        self.excl = excl


class Eng:
    def __init__(self, eng, sem, inorder=False):
        self.eng = eng
        self.sem = sem
        self.waited = {}
        self.inorder = inorder


class Arena:
    DT_SIZE = {F32: 4, BF16: 2}

    def __init__(self, nc, lo, hi):
        self.nc = nc
        self.free_list = [(lo, hi)]
        self.live = {}
        self.uid = 0

    def alloc(self, name, shape, dt=F32, top=False):
        n = 1
        for d in shape[1:]:
            n *= d
        size = (n * self.DT_SIZE[dt] + 63) // 64 * 64
        order = list(enumerate(self.free_list))
        if top:
            order = order[::-1]
        for i, (lo, hi) in order:
            if hi - lo >= size:
                if top:
                    off = hi - size
                    if hi - lo == size:
                        self.free_list.pop(i)
                    else:
                        self.free_list[i] = (lo, hi - size)
                else:
                    off = lo
                    if hi - lo == size:
                        self.free_list.pop(i)
                    else:
                        self.free_list[i] = (lo + size, hi)
                break
        else:
            raise MemoryError(f"SBUF arena full allocating {name} {size}: free={self.free_list} live={self.live}")
        self.uid += 1
        t = self.nc.alloc_sbuf_tensor_at(f"sb_{name}_{self.uid}", list(shape), dt, offset=off)
        self.live[name] = (off, size)
        return t

    def free(self, *names):
        for name in names:
            off, size = self.live.pop(name)
            self.free_list.append((off, off + size))
        self.free_list.sort()
        merged = []
        for lo, hi in self.free_list:
            if merged and merged[-1][1] == lo:
                merged[-1] = (merged[-1][0], hi)
            else:
                merged.append((lo, hi))
        self.free_list = merged


class K:
    def __init__(self, nc, es):
        self.nc = nc
        self.es = es
        self.nsem = 0
        self.PE = Eng(nc.tensor, self.new_sem("pe"), inorder=True)
        self.ACT = Eng(nc.scalar, self.new_sem("act"))
        self.DVE = Eng(nc.vector, self.new_sem("dve"))
        self.POOL = Eng(nc.gpsimd, self.new_sem("pool"))
        self.SP = Eng(nc.sync, self.new_sem("sp"))
        self.engines = [self.PE, self.ACT, self.DVE, self.POOL, self.SP]
        self.all_sems = []
        self.arena = Arena(nc, 16640, 229344 - 64)

    def new_sem(self, name):
        h = self.es.enter_context(self.nc.semaphore(f"s_{name}_{self.nsem}"))
        self.nsem += 1
        s = Sem(h, name)
        if hasattr(self, "all_sems"):
            self.all_sems.append(s)
        return s

    def _deps(self, rd, wr):
        deps = {}

        def add(tok):
            if tok is None:
                return
            s, v = tok
            if deps.get(s, 0) < v:
                deps[s] = v

        for b in rd:
            add(b.w)
            if b.excl:
                for s, v in b.rd.items():
                    add((s, v))
        for b in wr:
            add(b.w)
            for s, v in b.rd.items():
                add((s, v))
        return deps

    def _wait(self, E, deps):
        for s, v in deps.items():
            if v <= 0 or E.waited.get(s, 0) >= v:
                continue
            if s is E.sem and E.inorder:
                continue
            E.eng.wait_ge(s.h, v)
            E.waited[s] = v

    def op(self, E, fn, rd=(), wr=()):
        self._wait(E, self._deps(rd, wr))
        inst = fn()
        E.sem.v += 1
        inst.then_inc(E.sem.h, 1)
        tok = (E.sem, E.sem.v)
        for b in wr:
            b.w = tok
            b.rd = {}
        for b in rd:
            if b.rd.get(E.sem, 0) < E.sem.v:
                b.rd[E.sem] = E.sem.v
        return inst

    def dma(self, Q, sem, out, in_, rd=(), wr=()):
        deps = self._deps(rd, wr)
        if sem.v > 0 and deps.get(sem, 0) < sem.v:
            deps[sem] = sem.v
        self._wait(Q, deps)
        inst = Q.eng.dma_start(out=out, in_=in_)
        sem.v += 16
        inst.then_inc(sem.h, 16)
        tok = (sem, sem.v)
        for b in wr:
            b.w = tok
            b.rd = {}
        for b in rd:
            if b.rd.get(sem, 0) < sem.v:
                b.rd[sem] = sem.v
        return inst

    def barrier(self):
        sems = [e.sem for e in self.engines] + self.all_sems
        for E in self.engines:
            deps = {s: s.v for s in sems if s.v > 0}
            self._wait(E, deps)

    def finish(self, sems):
        deps = {s: s.v for s in sems if s.v > 0}
        self._wait(self.SP, deps)


class WStream:
    SLOT_ELEMS = 16 * 512

    def __init__(self, k, nslots):
        self.k = k
        self.n = nslots
        self.t = k.arena.alloc("wring", [128, nslots, self.SLOT_ELEMS], BF16)
        self.bufs = [Buf(f"wslot{i}") for i in range(nslots)]
        self.sems = [k.new_sem(f"w{i}") for i in range(nslots)]
        self.plan = []
        self.issued = 0
        self.acquired = 0
        self.released = 0

    def set_plan(self, plan):
        self.plan = plan

    def _issue(self):
        i = self.issued
        if i >= len(self.plan):
            return
        key, src, nkc, ncols = self.plan[i]
        s = i % self.n
        dst = self.t[:, s, 0:nkc * ncols].rearrange("p (k c) -> p k c", k=nkc)
        self.k.dma(self.k.POOL, self.sems[s], dst, src.rearrange("(k p) c -> p k c", p=128), wr=[self.bufs[s]])
        self.issued += 1

    def start(self):
        self._issue()
        self.k._wait(self.k.POOL, {self.sems[0]: self.sems[0].v})
        while self.issued < min(self.n, len(self.plan)):
            self._issue()

    def acquire(self, key):
        i = self.acquired
        pk, src, nkc, ncols = self.plan[i]
        assert pk == key, (pk, key, i)
        assert i < self.issued, "weight not issued"
        s = i % self.n
        self.acquired += 1
        ap = self.t[:, s, 0:nkc * ncols].rearrange("p (k c) -> p k c", k=nkc)
        return self.bufs[s], ap

    def release(self):
        self.released += 1
        while self.issued < min(self.released + self.n, len(self.plan)):
            self._issue()


def core_perm(j):
    own = [j, 7 - j, 8 + j, 15 - j]
    perm = []
    for s in range(4):
        grp = [4 * s + i for i in range(4)]
        others = [g for g in grp if g != own[s]]
        perm += others + [own[s]]
    return own, perm


def build_program(dbg=""):
    nc = bass.Bass("TRN2", target_bir_lowering=False)
    es = ExitStack()
    k = K(nc, es)

    needed = None
    if dbg and dbg[0] in "0A":
        needed = {"x_all", "w_in", "gains", "gqk", "pastb", "ident", "esel", "tri", "w_conv"}
    if dbg and dbg[0] == "B":
        needed = {"x_all", "x_halo", "w_in", "gains", "gqk", "pastb", "ident", "esel", "tri", "w_conv", "w_ao", "w_co", "w_o"}
    declared = []

    def din(name, shape, dt=F32):
        if needed is not None and name not in needed:
            return None
        declared.append(name)
        return nc.dram_tensor(name, list(shape), dt, kind="ExternalInput").ap()

    x_all = din("x_all", [S, D])
    x_halo = din("x_halo", [8, D])
    p_own = din("p_own", [T, PLE])
    w_in = din("w_in", [D, 10240])
    w_conv = din("w_conv", [128, 8, 3])
    w_ao = din("w_ao", [1024, D])
    w_co = din("w_co", [1024, D])
    w_o = din("w_o", [D, D])
    w_up = din("w_up", [D, DFF])
    w_down = din("w_down", [DFF, D])
    w_pg = din("w_pg", [D, D])
    w_pp = din("w_pp", [PLE, D])
    gains = din("gains", [128, 3, KC])
    gqk = din("gqk", [128, 2])
    pastb = din("pastb", [128, 2, 8, 16])
    ident_in = din("ident", [128, 128])
    esel_in = din("esel", [128, 16 * 128], BF16)
    tri_in = din("tri", [128, 2, 256], BF16)
    out = nc.dram_tensor("out", [T, D], F32, kind="ExternalOutput").ap()
    kt_s = nc.dram_tensor("kt_scratch", [H, 128, S], BF16, kind="Internal").ap()
    v_s = nc.dram_tensor("v_scratch", [S, H * HD], BF16, kind="Internal").ap()
    dbg_out = None
    if dbg:
        dbg_out = nc.dram_tensor("dbg", [128, 16384], F32, kind="ExternalOutput").ap()

    PE, ACT, DVE, POOL, SP = k.PE, k.ACT, k.DVE, k.POOL, k.SP

    def sb(name, shape, dt=F32):
        return k.arena.alloc(name, list(shape), dt)

    ident = sb("ident", [128, 128]);            b_ident = Buf("ident")
    ident_bf = sb("ident_bf", [128, 128], BF16); b_ident_bf = Buf("ident_bf")
    ones_bf = sb("ones_bf", [128, 128], BF16);  b_ones = Buf("ones")
    gains_sb = sb("gains", [128, 3, KC]);       b_gains = Buf("gains")
    gqk_sb = sb("gqk", [128, 2]);               b_gqk = Buf("gqk")
    pastb_sb = sb("pastb", [128, 2, 8, 16]);    b_pastb = Buf("pastb")
    wconv_sb = sb("wconv", [128, 8, 3]);        b_wconv = Buf("wconv")
    kmean = sb("kmean", [128, H, NBLK]);        b_kmean = [Buf(f"kmean{c}") for c in range(8)]
    kmean_bf = sb("kmean_bf", [128, H, NBLK], BF16); b_kmean_bf = Buf("kmean_bf")
    epsb = sb("epsb", [128, 1]);                b_eps = Buf("eps")

    psum = es.enter_context(nc.psum_tensor("psum", [128, 8, 512], F32))
    bank = [Buf(f"bank{i}", excl=True) for i in range(8)]

    c_sem = k.new_sem("const")

    def load_const(dst, src, b):
        k.dma(SP, k.new_sem("c"), dst, src, wr=[b])

    load_const(ident[:], ident_in[:], b_ident)
    load_const(gains_sb[:], gains[:], b_gains)
    load_const(gqk_sb[:], gqk[:], b_gqk)
    load_const(pastb_sb[:], pastb[:], b_pastb)
    load_const(wconv_sb[:], w_conv[:], b_wconv)
    k.op(DVE, lambda: nc.vector.memset(ones_bf[:], 1.0), wr=[b_ones])
    k.op(DVE, lambda: nc.vector.memset(epsb[:], EPS), wr=[b_eps])
    k.op(DVE, lambda: nc.vector.tensor_copy(out=ident_bf[:], in_=ident[:]), rd=[b_ident], wr=[b_ident_bf])

    ws = WStream(k, 3)
    h1T = sb("h1T", [128, KC, T], BF16);        b_h1T = [Buf(f"h1T{i}") for i in range(KC)]

    def wsrc(wap, r0, r1, c0, c1):
        return wap[r0:r1, c0:c1]

    plan = []
    plan.append(("wk0", wsrc(w_in, 0, D, 1024, 1536), 16, 512))
    plan.append(("wk1", wsrc(w_in, 0, D, 1536, 2048), 16, 512))
    plan.append(("wv0", wsrc(w_in, 0, D, 2048, 2560), 16, 512))
    if not (dbg and dbg[0] in "0A"):
        plan.append(("wq0", wsrc(w_in, 0, D, 0, 512), 16, 512))
        plan.append(("wq1", wsrc(w_in, 0, D, 512, 1024), 16, 512))
        for hf in range(2):
            plan.append((f"wcc{hf}", wsrc(w_in, 0, D, 4096 + hf * 512, 4096 + hf * 512 + 512), 16, 512))
            plan.append((f"wcx{hf}", wsrc(w_in, 0, D, 5120 + hf * 512, 5120 + hf * 512 + 512), 16, 512))
            plan.append((f"wcb{hf}", wsrc(w_in, 0, D, 3072 + hf * 512, 3072 + hf * 512 + 512), 16, 512))
        for cb in range(4):
            plan.append((f"wgc{cb}", wsrc(w_in, 0, D, 8192 + cb * 512, 8192 + cb * 512 + 512), 16, 512))
            plan.append((f"wco{cb}", wsrc(w_co, 0, 1024, cb * 512, cb * 512 + 512), 8, 512))
            plan.append((f"wga{cb}", wsrc(w_in, 0, D, 6144 + cb * 512, 6144 + cb * 512 + 512), 16, 512))
            plan.append((f"wao{cb}", wsrc(w_ao, 0, 1024, cb * 512, cb * 512 + 512), 8, 512))
        for cb in range(4):
            plan.append((f"wo{cb}", wsrc(w_o, 0, D, cb * 512, cb * 512 + 512), 16, 512))
        if not (dbg and dbg[0] == "B"):
            for j in range(8):
                for cbh in range(2):
                    c0 = j * 1024 + cbh * 512
                    plan.append((f"wup{j}_{cbh}", wsrc(w_up, 0, D, c0, c0 + 512), 16, 512))
                for cbh in range(2):
                    plan.append((f"wdn{j}_{cbh}", wsrc(w_down, j * 1024, (j + 1) * 1024, cbh * 1024, cbh * 1024 + 1024), 8, 1024))
            for cb in range(4):
                plan.append((f"wpg{cb}", wsrc(w_pg, 0, D, cb * 512, cb * 512 + 512), 16, 512))
    ws.set_plan(plan)
    ws.start()

    st = dict(nc=nc, k=k, es=es, ws=ws, psum=psum, bank=bank, dbg=dbg, dbg_out=dbg_out)
    st.update(locals())
    if dbg != "0":
        phase_a(st)
    if not (dbg and dbg[0] in "0A"):
        st.update(locals())
        phase_rest(st)

    k.barrier()
    es.close()
    nc._mk_declared = declared
    return nc


def phase_a(st):
    nc = st["nc"]; k = st["k"]; ws = st["ws"]; psum = st["psum"]; bank = st["bank"]
    PE, ACT, DVE, POOL, SP = k.PE, k.ACT, k.DVE, k.POOL, k.SP
    x_all = st["x_all"]; w_in = st["w_in"]; kt_s = st["kt_s"]; v_s = st["v_s"]
    ident = st["ident"]; b_ident = st["b_ident"]; ones_bf = st["ones_bf"]; b_ones = st["b_ones"]
    gains_sb = st["gains_sb"]; b_gains = st["b_gains"]; gqk_sb = st["gqk_sb"]; b_gqk = st["b_gqk"]
    kmean = st["kmean"]; b_kmean = st["b_kmean"]; epsb = st["epsb"]; b_eps = st["b_eps"]
    dbg = st["dbg"]; dbg_out = st["dbg_out"]
    h1T = st["h1T"]; b_h1T = st["b_h1T"]

    if True:
        local_names = []

        def sb(name, shape, dt=F32):
            local_names.append(name)
            return k.arena.alloc(name, list(shape), dt)

        NXT = 4
        xt = sb("xt", [128, NXT, D]);              b_xt = [Buf(f"xt{i}") for i in range(NXT)]
        s_xt = [k.new_sem(f"xt{i}") for i in range(NXT)]
        xTc = sb("xTc", [128, KC, 512]);           b_xTc = [Buf(f"xTc{i}") for i in range(KC)]
        hTc = sb("hTc", [128, KC, 512], BF16);     b_hTc = [Buf(f"hTc{i}") for i in range(KC)]
        wv1 = sb("wv1", [128, KC, 512], BF16);     b_wv1 = Buf("wv1")
        sq = sb("sq", [128, 4, 512], BF16);        b_sq = [Buf(f"sq{i}") for i in range(4)]
        rs = sb("rs", [128, 2, 512]);              b_rs = [Buf("rs0"), Buf("rs1")]
        kst = sb("kst", [128, 2, 512], BF16);      b_kst = [Buf("kst0"), Buf("kst1")]
        s_kst = [k.new_sem("kst0"), k.new_sem("kst1")]
        vst = sb("vst", [128, 1, 4, 1024], BF16);  b_vst = [Buf("vst0")]
        s_vst = [k.new_sem("vst0")]

        k.dma(POOL, k.new_sem("wv1"), wv1[:], w_in[:, 2560:3072].rearrange("(k p) c -> p k c", p=128), wr=[b_wv1])

        b_wk0, wk0 = ws.acquire("wk0")
        b_wk1, wk1 = ws.acquire("wk1")
        b_wv0, wv0 = ws.acquire("wv0")

        xtile_i = [0]

        def load_xtile(tile_idx):
            i = tile_idx % NXT
            k.dma(SP, s_xt[i], xt[:, i, :], x_all[tile_idx * 128:(tile_idx + 1) * 128, :], wr=[b_xt[i]])

        NT = S // 128
        for t0 in range(NXT):
            load_xtile(t0)
        nl = [NXT]
        ginv = sb("ginv", [128, KC]); b_ginv = Buf("ginv")
        k.op(DVE, lambda: nc.vector.reciprocal(out=ginv[:], in_=gains_sb[:, 0, :]), rd=[b_gains], wr=[b_ginv])
        sqc = [0]
        nchunks = 1 if (dbg and dbg[0] == "A" and len(dbg) > 1) else S // 512

        NSQ = 4
        rot = [3, 4, 6, 7]
        roti = [0]
        sq_of = {}

        def emit_Ttrans(c, kc):
            bk = kc % 2
            for t in range(4):
                ti = c * 4 + t
                xi = ti % NXT
                k.op(PE, lambda t=t, xi=xi: nc.tensor.transpose(
                    psum[:, bk, t * 128:(t + 1) * 128], xt[:, xi, kc * 128:(kc + 1) * 128], ident[:]),
                    rd=[b_xt[xi], b_ident], wr=[bank[bk]])
            k.op(DVE, lambda: nc.vector.tensor_scalar(out=xTc[:, kc, :], in0=psum[:, bk, :], scalar1=gains_sb[:, 0, kc:kc + 1],
                                                      scalar2=None, op0=ALU.mult),
                 rd=[bank[bk], b_gains], wr=[b_xTc[kc]])
            si = sqc[0] % NSQ
            sqc[0] += 1
            sq_of[("n", c, kc)] = si
            k.op(ACT, lambda: nc.scalar.activation(out=sq[:, si, :], in_=xTc[:, kc, :], func=AF.Square, scale=ginv[:, kc:kc + 1]),
                 rd=[b_xTc[kc], b_ginv], wr=[b_sq[si]])

        def emit_Tsum(c, kc):
            si = sq_of.pop(("n", c, kc))
            k.op(PE, lambda: nc.tensor.matmul(psum[:, 2, :], lhsT=ones_bf[:], rhs=sq[:, si, :], start=(kc == 0), stop=(kc == KC - 1)),
                 rd=[b_ones, b_sq[si]], wr=[bank[2]])
            if kc == KC - 1:
                for t in range(4):
                    if nl[0] < NT:
                        load_xtile(nl[0])
                        nl[0] += 1
                k.op(ACT, lambda: nc.scalar.activation(out=rs[:, 0, :], in_=psum[:, 2, :], func=AF.Sqrt, bias=epsb[:], scale=1.0 / D),
                     rd=[bank[2], b_eps], wr=[b_rs[0]])
                k.op(DVE, lambda: nc.vector.reciprocal(out=rs[:, 0, :], in_=rs[:, 0, :]), rd=[b_rs[0]], wr=[b_rs[0]])

        b_hTa = [Buf(f"hTa{i}") for i in range(KC)]

        def hbuf(c):
            if c % 2 == 0 and c <= 4:
                return (lambda kc: h1T[:, kc, 512:1024]), b_hTa
            return (lambda kc: hTc[:, kc, :]), b_hTc

        def emit_hT(c):
            hf, hb_ = hbuf(c)
            for kc in range(KC):
                if kc % 3 != 2:
                    k.op(DVE, lambda kc=kc: nc.vector.tensor_tensor(out=hf(kc), in0=xTc[:, kc, :], in1=rs[:, 0, :], op=ALU.mult),
                         rd=[b_xTc[kc], b_rs[0]], wr=[hb_[kc]])
                else:
                    k.op(POOL, lambda kc=kc: nc.gpsimd.tensor_tensor(out=hf(kc), in0=xTc[:, kc, :], in1=rs[:, 0, :], op=ALU.mult),
                         rd=[b_xTc[kc], b_rs[0]], wr=[hb_[kc]])
            if c % 2 == 1:
                so = (c - 1) // 2
                for kc in range(KC):
                    k.op(POOL, lambda kc=kc, so=so: nc.gpsimd.tensor_copy(out=h1T[:, kc, so * 256:(so + 1) * 256], in_=hTc[:, kc, 256:512]),
                         rd=[b_hTc[kc]], wr=[b_h1T[kc], b_hTa[kc]])

        kbank = {}

        def emit_Kproj(c, h):
            wb, wt = (b_wk0, wk0) if h < 4 else (b_wk1, wk1)
            hc = (h % 4) * 128
            pb = rot[roti[0] % 4]
            roti[0] += 1
            kbank[h] = pb
            hf, hb_ = hbuf(c)
            for g in range(4):
                def kproj(g=g):
                    for kc in range(4 * g, 4 * g + 4):
                        inst = nc.tensor.matmul(psum[:, pb, :], lhsT=wt[:, kc, hc:hc + 128], rhs=hf(kc),
                                                start=(kc == 0), stop=(kc == KC - 1))
                    return inst
                k.op(PE, kproj, rd=[wb] + hb_[4 * g:4 * g + 4], wr=[bank[pb]])
            si = sqc[0] % NSQ
            sqc[0] += 1
            sq_of[("k", c, h)] = si
            k.op(ACT, lambda: nc.scalar.activation(out=sq[:, si, :], in_=psum[:, pb, :], func=AF.Square), rd=[bank[pb]], wr=[b_sq[si]])

        def emit_Kfin(c, h):
            pb = kbank[h]
            si = sq_of.pop(("k", c, h))
            k.op(PE, lambda: nc.tensor.matmul(psum[:, 5, :], lhsT=ones_bf[:], rhs=sq[:, si, :], start=True, stop=True),
                 rd=[b_ones, b_sq[si]], wr=[bank[5]])
            k.op(ACT, lambda: nc.scalar.activation(out=rs[:, 1, :], in_=psum[:, 5, :], func=AF.Sqrt, bias=epsb[:], scale=1.0 / HD),
                 rd=[bank[5], b_eps], wr=[b_rs[1]])
            k.op(DVE, lambda: nc.vector.reciprocal(out=rs[:, 1, :], in_=rs[:, 1, :]), rd=[b_rs[1]], wr=[b_rs[1]])
            ki = h % 2
            k.op(DVE, lambda: nc.vector.scalar_tensor_tensor(out=kst[:, ki, :], in0=psum[:, pb, :], scalar=gqk_sb[:, 1:2], in1=rs[:, 1, :],
                                                             op0=ALU.mult, op1=ALU.mult),
                 rd=[bank[pb], b_gqk, b_rs[1]], wr=[b_kst[ki]])
            k.op(DVE, lambda: nc.vector.tensor_reduce(out=kmean[:, h, 2 * c:2 * c + 2], in_=kst[:, ki, :].rearrange("p (b t) -> p b t", b=2),
                                                      op=ALU.add, axis=AX.X),
                 rd=[b_kst[ki]], wr=[b_kmean[c]])
            k.dma(SP, s_kst[ki], kt_s[h, :, c * 512:(c + 1) * 512], kst[:, ki, :], rd=[b_kst[ki]])

        def emit_V(c, t, cb):
            wb, wt = (b_wv0, wv0) if cb == 0 else (b_wv1, wv1)
            pb = rot[roti[0] % 4]
            roti[0] += 1

            hf, hb_ = hbuf(c)

            def vproj():
                for kc in range(KC):
                    inst = nc.tensor.matmul(psum[:, pb, :], lhsT=hf(kc)[:, t * 128:(t + 1) * 128], rhs=wt[:, kc, :],
                                            start=(kc == 0), stop=(kc == KC - 1))
                return inst
            k.op(PE, vproj, rd=[wb] + hb_, wr=[bank[pb]])
            k.op(ACT, lambda: nc.scalar.copy(out=vst[:, 0, t, cb * 512:(cb + 1) * 512], in_=psum[:, pb, :]), rd=[bank[pb]], wr=[b_vst[0]])
            if t == 3 and cb == 1:
                k.dma(SP, s_vst[0], v_s[c * 512:(c + 1) * 512, :].rearrange("(t p) c -> p t c", p=128), vst[:, 0, :, :], rd=[b_vst[0]])

        early_release = not dbg
        for kc in range(KC):
            emit_Ttrans(0, kc)
            if kc >= 1:
                emit_Tsum(0, kc - 1)
        emit_Tsum(0, KC - 1)
        emit_hT(0)
        for c in range(nchunks):
            nxt = c + 1 < nchunks
            for i in range(KC):
                if dbg != "A1":
                    if i < H:
                        emit_Kproj(c, i)
                    elif dbg != "A2":
                        emit_V(c, (i - H) // 2, (i - H) % 2)
                if early_release and c == nchunks - 1 and i in (3, 7, 14):
                    ws.release()
                if nxt:
                    emit_Ttrans(c + 1, i)
                    if i >= 1:
                        emit_Tsum(c + 1, i - 1)
                if dbg != "A1" and 1 <= i <= H:
                    emit_Kfin(c, i - 1)
            if nxt:
                emit_Tsum(c + 1, KC - 1)
                emit_hT(c + 1)

        if dbg and dbg[0] == "A":
            dbg_out = st["dbg_out"]
            k.barrier()
            dt_ = sb("dbgt", [128, 2048])
            bd = Buf("dbgt")
            k.op(DVE, lambda: nc.vector.memset(dt_[:], 0.0), wr=[bd])
            k.op(DVE, lambda: nc.vector.tensor_copy(out=dt_[:, 0:512], in_=hTc[:, 0, :]), rd=[b_hTc[0]], wr=[bd])
            if dbg == "A":
                k.op(DVE, lambda: nc.vector.tensor_copy(out=dt_[:, 512:640], in_=kmean[:].rearrange("p h n -> p (h n)")), rd=b_kmean, wr=[bd])
            k.op(DVE, lambda: nc.vector.tensor_copy(out=dt_[:, 1024:1536], in_=rs[:, 0, :]), rd=[b_rs[0]], wr=[bd])
            k.dma(SP, k.new_sem("dbg"), dbg_out[:, 0:2048], dt_[:], rd=[bd])
            k.barrier()
        k.barrier()
        k.arena.free(*local_names)
        if not early_release:
            ws.release(); ws.release(); ws.release()


def phase_rest(st):
    nc = st["nc"]; k = st["k"]; ws = st["ws"]; psum = st["psum"]; bank = st["bank"]
    PE, ACT, DVE, POOL, SP = k.PE, k.ACT, k.DVE, k.POOL, k.SP
    x_all = st["x_all"]; x_halo = st["x_halo"]; p_own = st["p_own"]; out = st["out"]
    kt_s = st["kt_s"]; v_s = st["v_s"]
    ident = st["ident"]; b_ident = st["b_ident"]; ident_bf = st["ident_bf"]; b_ident_bf = st["b_ident_bf"]
    ones_bf = st["ones_bf"]; b_ones = st["b_ones"]
    esel_in = st["esel_in"]; tri_in = st["tri_in"]
    gains_sb = st["gains_sb"]; b_gains = st["b_gains"]; gqk_sb = st["gqk_sb"]; b_gqk = st["b_gqk"]
    pastb_sb = st["pastb_sb"]; b_pastb = st["b_pastb"]; wconv_sb = st["wconv_sb"]; b_wconv = st["b_wconv"]
    kmean = st["kmean"]; b_kmean = st["b_kmean"]; kmean_bf = st["kmean_bf"]; b_kmean_bf = st["b_kmean_bf"]
    epsb = st["epsb"]; b_eps = st["b_eps"]
    h1T = st["h1T"]; b_h1T = st["b_h1T"]
    dbg = st["dbg"]; dbg_out = st["dbg_out"]
    SCALE = float(HD) ** -0.5

    cnt = {"sq": 0}

    def fm_norm(es_sq, es_rs, src_fn, src_bufs, ncols, nkc, inv_n, gain_fn, dst_fn, dst_bufs, nb_sum, tmp_rs, b_tmp_rs):
        sq, b_sq = es_sq
        for kc in range(nkc):
            si = cnt["sq"] % 3
            cnt["sq"] += 1
            k.op(ACT, lambda kc=kc, si=si: nc.scalar.activation(out=sq[:, si, 0:ncols], in_=src_fn(kc), func=AF.Square),
                 rd=[src_bufs[kc]], wr=[b_sq[si]])
            k.op(PE, lambda kc=kc, si=si: nc.tensor.matmul(psum[:, nb_sum, 0:ncols], lhsT=ones_bf[:], rhs=sq[:, si, 0:ncols],
                                                             start=(kc == 0), stop=(kc == nkc - 1)),
                 rd=[b_ones, b_sq[si]], wr=[bank[nb_sum]])
        k.op(ACT, lambda: nc.scalar.activation(out=tmp_rs[:, 0:ncols], in_=psum[:, nb_sum, 0:ncols], func=AF.Sqrt,
                                               bias=epsb[:], scale=inv_n),
             rd=[bank[nb_sum], b_eps], wr=[b_tmp_rs])
        k.op(DVE, lambda: nc.vector.reciprocal(out=tmp_rs[:, 0:ncols], in_=tmp_rs[:, 0:ncols]), rd=[b_tmp_rs], wr=[b_tmp_rs])
        for kc in range(nkc):
            k.op(DVE, lambda kc=kc: nc.vector.scalar_tensor_tensor(
                out=dst_fn(kc), in0=src_fn(kc), scalar=gain_fn(kc), in1=tmp_rs[:, 0:ncols], op0=ALU.mult, op1=ALU.mult),
                rd=[src_bufs[kc], b_gains, b_gqk, b_tmp_rs], wr=[dst_bufs[kc]])

    def mm_group(pb, ncols, pairs, rd, col0=0):
        def fn():
            n = len(pairs)
            for i, (l, r) in enumerate(pairs):
                inst = nc.tensor.matmul(psum[:, pb, col0:col0 + ncols], lhsT=l, rhs=r, start=(i == 0), stop=(i == n - 1))
            return inst
        k.op(PE, fn, rd=rd, wr=[bank[pb]])

    A = k.arena

    def sbB(name, shape, dt=F32, top=False):
        return A.alloc(name, list(shape), dt, top=top)

    if True:
        sq = sbB("sq2", [128, 3, 512], BF16, top=True);       b_sq = [Buf(f"sq2_{i}") for i in range(3)]
        rs = sbB("rs2", [128, 512], top=True);                b_rs = Buf("rs2")
        esel = sbB("esel", [128, 16 * 128], BF16, top=True);  b_esel = Buf("esel")
        tri = sbB("tri", [128, 2, 256], BF16, top=True);      b_tri = Buf("tri")
        k.dma(SP, k.new_sem("c"), esel[:], esel_in[:], wr=[b_esel])
        k.dma(SP, k.new_sem("c"), tri[:], tri_in[:], wr=[b_tri])

        QT = sbB("QT", [128, H, T], BF16);          b_QT = [Buf(f"QT{h}") for h in range(H)]
        attnT = sbB("attnT", [128, H, T], BF16);    b_attnT = [Buf(f"attnT{h}") for h in range(H)]
        hhT = sbB("hhT", [128, KC, 8], BF16);       b_hhT = [Buf(f"hhT{i}") for i in range(KC)]
        MBT = sbB("MBT", [128, H, T], BF16);        b_MBT = [Buf(f"MBT{h}") for h in range(H)]

        if True:
            xh = sbB("xh", [8, D], F32); b_xh = Buf("xh")
            xhT = sbB("xhT", [128, KC, 8], F32); b_xhT = [Buf(f"xhT{i}") for i in range(KC)]
            k.dma(SP, k.new_sem("xh"), xh[:], x_halo[:], wr=[b_xh])
            for kc in range(KC):
                k.op(PE, lambda kc=kc: nc.tensor.transpose(psum[:, 0, kc * 8:(kc + 1) * 8], xh[:, kc * 128:(kc + 1) * 128], ident[0:8, 0:8]),
                     rd=[b_xh, b_ident], wr=[bank[0]])
            k.op(DVE, lambda: nc.vector.tensor_copy(out=xhT[:].rearrange("p k t -> p (k t)"), in_=psum[:, 0, 0:128]),
                 rd=[bank[0]], wr=b_xhT)
            fm_norm((sq, b_sq), None, lambda kc: xhT[:, kc, :], b_xhT, 8, KC, 1.0 / D,
                    lambda kc: gains_sb[:, 0, kc:kc + 1], lambda kc: hhT[:, kc, :], b_hhT, 1, rs, b_rs)
            k.op(DVE, lambda: nc.vector.tensor_scalar(out=kmean_bf[:], in0=kmean[:], scalar1=1.0 / BLK, scalar2=None, op0=ALU.mult),
                 rd=b_kmean, wr=[b_kmean_bf])
            k.op(DVE, lambda: nc.vector.memset(MBT[:], 0.0), wr=b_MBT)
            k.barrier()
            A.free("xh", "xhT")

        rotq = [2, 3, 6, 7]
        qstate = {}

        def q_proj(idx, h, half, wb, wt, hh):
            pb = rotq[idx % 4]
            mm_group(pb, 512, [(wt[:, kc, hh * 128:(hh + 1) * 128], h1T[:, kc, half * 512:(half + 1) * 512]) for kc in range(KC)],
                     rd=[wb] + b_h1T)
            si = cnt["sq"] % 3
            cnt["sq"] += 1
            k.op(ACT, lambda: nc.scalar.activation(out=sq[:, si, :], in_=psum[:, pb, :], func=AF.Square), rd=[bank[pb]], wr=[b_sq[si]])
            qstate[idx] = (pb, si, h, half)

        def q_fin(idx):
            pb, si, h, half = qstate.pop(idx)
            k.op(PE, lambda: nc.tensor.matmul(psum[:, 4, :], lhsT=ones_bf[:], rhs=sq[:, si, :], start=True, stop=True),
                 rd=[b_ones, b_sq[si]], wr=[bank[4]])
            k.op(ACT, lambda: nc.scalar.activation(out=rs[:], in_=psum[:, 4, :], func=AF.Sqrt, bias=epsb[:], scale=1.0 / HD),
                 rd=[bank[4], b_eps], wr=[b_rs])
            k.op(DVE, lambda: nc.vector.reciprocal(out=rs[:], in_=rs[:]), rd=[b_rs], wr=[b_rs])
            k.op(DVE, lambda: nc.vector.scalar_tensor_tensor(
                out=QT[:, h, half * 512:(half + 1) * 512], in0=psum[:, pb, :], scalar=gqk_sb[:, 0:1], in1=rs[:],
                op0=ALU.mult, op1=ALU.mult),
                rd=[bank[pb], b_gqk, b_rs], wr=[b_QT[h]])

        idx = 0
        for qh in range(2):
            wb, wt = ws.acquire(f"wq{qh}")
            for hh in range(4):
                for half in range(2):
                    q_proj(idx, qh * 4 + hh, half, wb, wt, hh)
                    if idx >= 1:
                        q_fin(idx - 1)
                    idx += 1
            ws.release()
        q_fin(idx - 1)

        if True:
            gb = sbB("gb", [128, H, 8, 16], F32); b_gb = [Buf(f"gb{h}") for h in range(H)]
            m8 = sbB("m8", [128, H, 8, 8], F32); b_m8 = [Buf(f"m8{h}") for h in range(H)]
            mb = sbB("mb", [128, H, 8, 16], F32); b_mb = [Buf(f"mb{h}") for h in range(H)]
            for h in range(H):
                pbg = 5 + h // 4
                c0 = (h % 4) * 128

                def gate_mm(h=h, pbg=pbg, c0=c0):
                    for tt in range(8):
                        inst = nc.tensor.matmul(psum[:, pbg, c0 + tt * 16:c0 + (tt + 1) * 16], lhsT=QT[:, h, tt * 128:(tt + 1) * 128],
                                                rhs=kmean_bf[:, h, :], start=True, stop=True)
                    return inst
                k.op(PE, gate_mm, rd=[b_QT[h], b_kmean_bf], wr=[bank[pbg]])
            for h in range(H):
                pbg = 5 + h // 4
                c0 = (h % 4) * 128
                k.op(DVE, lambda h=h, pbg=pbg, c0=c0: nc.vector.tensor_tensor(out=gb[:, h].rearrange("p a b -> p (a b)"), in0=psum[:, pbg, c0:c0 + 128],
                                                                              in1=pastb_sb[:, 0].rearrange("p a b -> p (a b)"), op=ALU.add),
                     rd=[bank[pbg], b_pastb], wr=[b_gb[h]])
                for tt in range(8):
                    k.op(DVE, lambda h=h, tt=tt: nc.vector.max(out=m8[:, h, tt, :], in_=gb[:, h, tt, :]), rd=[b_gb[h]], wr=[b_m8[h]])
                for tt in range(8):
                    k.op(DVE, lambda h=h, tt=tt: nc.vector.tensor_scalar(out=mb[:, h, tt, :], in0=gb[:, h, tt, :], scalar1=m8[:, h, tt, 2:3], scalar2=-NEG,
                                                                         op0=ALU.is_ge, op1=ALU.mult),
                         rd=[b_gb[h], b_m8[h]], wr=[b_mb[h]])
                k.op(DVE, lambda h=h: nc.vector.scalar_tensor_tensor(out=mb[:, h].rearrange("p a b -> p (a b)"), in0=mb[:, h].rearrange("p a b -> p (a b)"),
                                                                     scalar=NEG, in1=pastb_sb[:, 1].rearrange("p a b -> p (a b)"),
                                                                     op0=ALU.add, op1=ALU.add),
                     rd=[b_mb[h], b_pastb], wr=[b_mb[h]])
                pb0 = 0 if h % 2 == 0 else 2
                for tt in range(8):
                    pbt = pb0 + (tt // 4)
                    k.op(PE, lambda h=h, tt=tt, pbt=pbt: nc.tensor.transpose(psum[0:16, pbt, (tt % 4) * 128:(tt % 4 + 1) * 128], mb[:, h, tt, :], ident[:]),
                         rd=[b_mb[h], b_ident], wr=[bank[pbt]])
                for hf in range(2):
                    k.op(ACT, lambda h=h, hf=hf, pb0=pb0: nc.scalar.copy(out=MBT[0:16, h, hf * 512:(hf + 1) * 512], in_=psum[0:16, pb0 + hf, :]),
                         rd=[bank[pb0 + hf]], wr=[b_MBT[h]])
            k.barrier()
            A.free("gb", "m8", "mb")

        if True:
            KTh = sbB("KTh", [128, 2, S], BF16); b_KTh = [Buf("KTh0"), Buf("KTh1")]
            Vh = sbB("Vh", [128, 2, 32, HD], BF16); b_Vh = [Buf("Vh0"), Buf("Vh1")]
            s_K = [k.new_sem("KTh0"), k.new_sem("KTh1")]
            s_V = [k.new_sem("Vh0"), k.new_sem("Vh1")]
            PT = sbB("PT", [128, 4, 512], BF16); b_PT = [Buf(f"PT{i}") for i in range(4)]
            rden = sbB("rden", [128, 256], F32); b_rden = Buf("rden")

            def load_kv(h):
                i = h % 2
                k.dma(SP, s_K[i], KTh[:, i, :], kt_s[h, :, :], wr=[b_KTh[i]])
                k.dma(SP, s_V[i], Vh[:, i, :, :], v_s[:, h * HD:(h + 1) * HD].rearrange("(t p) d -> p t d", p=128), wr=[b_Vh[i]])

            load_kv(0)
            pti = 0
            for h in range(H):
                if h + 1 < H:
                    load_kv(h + 1)
                hi = h % 2
                for s_ in range(4):
                    nblk = 4 * s_ + 4
                    q_ap = QT[:, h, s_ * 256:(s_ + 1) * 256]
                    pb_o = 3 + (s_ % 2)
                    pb_d = 5 + (s_ % 2)

                    def emit_qk(n, pbs):
                        def fn():
                            for i in range(2):
                                kt = 2 * n + i
                                nc.tensor.matmul(psum[:, pbs, i * 256:(i + 1) * 256], lhsT=KTh[:, hi, kt * 128:(kt + 1) * 128], rhs=q_ap,
                                                 start=True, stop=False)
                                if n < nblk - 1:
                                    inst = nc.tensor.matmul(psum[:, pbs, i * 256:(i + 1) * 256], lhsT=esel[:, n * 128:(n + 1) * 128],
                                                            rhs=MBT[:, h, s_ * 256:(s_ + 1) * 256], start=False, stop=True)
                                else:
                                    inst = nc.tensor.matmul(psum[:, pbs, i * 256:(i + 1) * 256], lhsT=ident_bf[:], rhs=tri[:, i, :],
                                                            start=False, stop=True)
                            return inst
                        k.op(PE, fn, rd=[b_KTh[hi], b_QT[h], b_esel, b_MBT[h], b_ident_bf, b_tri], wr=[bank[pbs]])

                    def emit_exp(n, pbs, pi):
                        k.op(ACT, lambda: nc.scalar.activation(out=PT[:, pi, :], in_=psum[:, pbs, :], func=AF.Exp, scale=SCALE),
                             rd=[bank[pbs]], wr=[b_PT[pi]])

                    def emit_pv(n, pi):
                        def fn():
                            for i in range(2):
                                kt = 2 * n + i
                                nc.tensor.matmul(psum[:, pb_o, 0:256], lhsT=Vh[:, hi, kt, :], rhs=PT[:, pi, i * 256:(i + 1) * 256],
                                                 start=(n == 0 and i == 0), stop=(n == nblk - 1 and i == 1))
                                inst = nc.tensor.matmul(psum[:, pb_d, 0:256], lhsT=ones_bf[:], rhs=PT[:, pi, i * 256:(i + 1) * 256],
                                                        start=(n == 0 and i == 0), stop=(n == nblk - 1 and i == 1))
                            return inst
                        k.op(PE, fn, rd=[b_Vh[hi], b_PT[pi], b_ones], wr=[bank[pb_o], bank[pb_d]])

                    LOOK = 2
                    slots = {}
                    for n in range(min(LOOK, nblk)):
                        pbs = n % 3
                        emit_qk(n, pbs)
                    for n in range(nblk):
                        pbs = n % 3
                        pi = pti % 4
                        pti += 1
                        emit_exp(n, pbs, pi)
                        if n + LOOK < nblk:
                            emit_qk(n + LOOK, (n + LOOK) % 3)
                        emit_pv(n, pi)
                    k.op(DVE, lambda pb_d=pb_d: nc.vector.reciprocal(out=rden[:], in_=psum[:, pb_d, 0:256]), rd=[bank[pb_d]], wr=[b_rden])
                    k.op(DVE, lambda pb_o=pb_o, h=h, s_=s_: nc.vector.tensor_tensor(out=attnT[:, h, s_ * 256:(s_ + 1) * 256], in0=psum[:, pb_o, 0:256],
                                                                                     in1=rden[:], op=ALU.mult),
                         rd=[bank[pb_o], b_rden], wr=[b_attnT[h]])
            k.barrier()
            A.free("KTh", "Vh", "PT", "rden", "QT", "MBT", "esel", "tri")

        if dbg == "B3":
            if True:
                dt_ = sbB("dbgt", [128, 4096], F32); bd = Buf("dbgt")
                k.op(DVE, lambda: nc.vector.memset(dt_[:], 0.0), wr=[bd])
                k.op(DVE, lambda: nc.vector.tensor_copy(out=dt_[:, 0:1024], in_=QT[:, 0, :]), rd=b_QT, wr=[bd])
                k.op(DVE, lambda: nc.vector.tensor_copy(out=dt_[:, 1024:2048], in_=attnT[:, 0, :]), rd=b_attnT, wr=[bd])
                k.op(DVE, lambda: nc.vector.tensor_copy(out=dt_[:, 2048:3072], in_=MBT[:, 0, :]), rd=b_MBT, wr=[bd])
                k.op(DVE, lambda: nc.vector.tensor_copy(out=dt_[:, 3072:4096], in_=attnT[:, 7, :]), rd=b_attnT, wr=[bd])
                k.dma(SP, k.new_sem("dbg"), dbg_out[:, 0:4096], dt_[:], rd=[bd])
                k.barrier()
                return

        mergedT = sbB("mergedT", [128, KC, T], BF16, top=True); b_mg = [Buf(f"mg{i}") for i in range(KC)]
        vT = sbB("vT", [128, 8, T], BF16);          b_vT = [Buf(f"vT{i}") for i in range(8)]
        if True:
            ccs = sbB("ccs", [128, 4, T + 8], F32); b_ccs = [Buf(f"ccs{i}") for i in range(4)]
            upad = sbB("upad", [128, 4, 4, 258], F32); b_upad = [Buf(f"upad{i}") for i in range(4)]
            for hf in range(2):
                wb, wt = ws.acquire(f"wcc{hf}")
                for ch in range(4):
                    for half in range(2):
                        pb = (ch * 2 + half) % 2
                        mm_group(pb, 512, [(wt[:, kc, ch * 128:(ch + 1) * 128], h1T[:, kc, half * 512:(half + 1) * 512]) for kc in range(KC)],
                                 rd=[wb] + b_h1T)
                        k.op(ACT, lambda ch=ch, half=half, pb=pb: nc.scalar.copy(out=ccs[:, ch, half * 512:(half + 1) * 512], in_=psum[:, pb, :]),
                             rd=[bank[pb]], wr=[b_ccs[ch]])
                    mm_group(2, 8, [(wt[:, kc, ch * 128:(ch + 1) * 128], hhT[:, kc, :]) for kc in range(KC)], rd=[wb] + b_hhT)
                    k.op(ACT, lambda ch=ch: nc.scalar.copy(out=ccs[:, ch, T:T + 8], in_=psum[:, 2, 0:8]), rd=[bank[2]], wr=[b_ccs[ch]])
                ws.release()
                wb, wt = ws.acquire(f"wcx{hf}")
                for ch in range(4):
                    for half in range(2):
                        pb = (ch * 2 + half) % 2
                        mm_group(pb, 512, [(wt[:, kc, ch * 128:(ch + 1) * 128], h1T[:, kc, half * 512:(half + 1) * 512]) for kc in range(KC)],
                                 rd=[wb] + b_h1T)
                        k.op(DVE, lambda ch=ch, half=half, pb=pb: nc.vector.tensor_tensor(
                            out=upad[:, ch, 2 * half:2 * half + 2, 2:258], in0=psum[:, pb, :].rearrange("p (b t) -> p b t", b=2),
                            in1=ccs[:, ch, half * 512:(half + 1) * 512].rearrange("p (b t) -> p b t", b=2), op=ALU.mult),
                            rd=[bank[pb], b_ccs[ch]], wr=[b_upad[ch]])
                    mm_group(2, 8, [(wt[:, kc, ch * 128:(ch + 1) * 128], hhT[:, kc, :]) for kc in range(KC)], rd=[wb] + b_hhT)
                    k.op(DVE, lambda ch=ch: nc.vector.tensor_tensor(
                        out=upad[:, ch, :, 0:2], in0=psum[:, 2, 0:8].rearrange("p (b t) -> p b t", b=4),
                        in1=ccs[:, ch, T:T + 8].rearrange("p (b t) -> p b t", b=4), op=ALU.mult),
                        rd=[bank[2], b_ccs[ch]], wr=[b_upad[ch]])
                    gch = hf * 4 + ch
                    acc = ccs[:, ch, 0:T].rearrange("p (b t) -> p b t", b=4)
                    k.op(DVE, lambda ch=ch, gch=gch, acc=acc: nc.vector.tensor_scalar(out=acc, in0=upad[:, ch, :, 2:258], scalar1=wconv_sb[:, gch, 2:3],
                                                                                    scalar2=None, op0=ALU.mult),
                         rd=[b_upad[ch], b_wconv], wr=[b_ccs[ch]])
                    for tap in (1, 0):
                        k.op(DVE, lambda ch=ch, gch=gch, acc=acc, tap=tap: nc.vector.scalar_tensor_tensor(
                            out=acc, in0=upad[:, ch, :, tap:tap + 256], scalar=wconv_sb[:, gch, tap:tap + 1], in1=acc,
                            op0=ALU.mult, op1=ALU.add),
                            rd=[b_upad[ch], b_wconv, b_ccs[ch]], wr=[b_ccs[ch]])
                ws.release()
                wb, wt = ws.acquire(f"wcb{hf}")
                for ch in range(4):
                    gch = hf * 4 + ch
                    for half in range(2):
                        pb = (ch * 2 + half) % 2
                        mm_group(pb, 512, [(wt[:, kc, ch * 128:(ch + 1) * 128], h1T[:, kc, half * 512:(half + 1) * 512]) for kc in range(KC)],
                                 rd=[wb] + b_h1T)
                        k.op(DVE, lambda ch=ch, gch=gch, half=half, pb=pb: nc.vector.tensor_tensor(
                            out=vT[:, gch, half * 512:(half + 1) * 512], in0=psum[:, pb, :], in1=ccs[:, ch, half * 512:(half + 1) * 512], op=ALU.mult),
                            rd=[bank[pb], b_ccs[ch]], wr=[b_vT[gch]])
                ws.release()
            k.barrier()
            A.free("ccs", "upad", "hhT")

        if True:
            sg = sbB("sg", [128, 2, 4, T], BF16); b_sg = [[Buf(f"sg{a}_{i}") for i in range(4)] for a in range(2)]
            m1 = sbB("m1", [128, 4, T], F32); b_m1 = [Buf(f"m1_{i}") for i in range(4)]
            for cb in range(4):
                for which, (gname, wname, srcT, b_src, nk) in enumerate(((f"wgc{cb}", f"wco{cb}", vT, b_vT, 8), (f"wga{cb}", f"wao{cb}", attnT, b_attnT, 8))):
                    wb, wt = ws.acquire(gname)
                    for ch in range(4):
                        for half in range(2):
                            pb = (ch * 2 + half) % 2
                            mm_group(pb, 512, [(wt[:, kc, ch * 128:(ch + 1) * 128], h1T[:, kc, half * 512:(half + 1) * 512]) for kc in range(KC)],
                                     rd=[wb] + b_h1T)
                            k.op(ACT, lambda which=which, ch=ch, half=half, pb=pb: nc.scalar.activation(
                                out=sg[:, which, ch, half * 512:(half + 1) * 512], in_=psum[:, pb, :], func=AF.Sigmoid),
                                rd=[bank[pb]], wr=[b_sg[which][ch]])
                    ws.release()
                    wb, wt = ws.acquire(wname)
                    for ch in range(4):
                        gch = cb * 4 + ch
                        for half in range(2):
                            pb = 2 + (ch * 2 + half) % 2
                            mm_group(pb, 512, [(wt[:, kc, ch * 128:(ch + 1) * 128], srcT[:, kc, half * 512:(half + 1) * 512]) for kc in range(nk)],
                                     rd=[wb] + b_src)
                            if which == 0:
                                k.op(DVE, lambda ch=ch, half=half, pb=pb: nc.vector.tensor_tensor(
                                    out=m1[:, ch, half * 512:(half + 1) * 512], in0=psum[:, pb, :], in1=sg[:, 0, ch, half * 512:(half + 1) * 512], op=ALU.mult),
                                    rd=[bank[pb], b_sg[0][ch]], wr=[b_m1[ch]])
                            else:
                                k.op(DVE, lambda ch=ch, half=half, pb=pb: nc.vector.tensor_tensor(
                                    out=sg[:, 1, ch, half * 512:(half + 1) * 512], in0=psum[:, pb, :], in1=sg[:, 1, ch, half * 512:(half + 1) * 512], op=ALU.mult),
                                    rd=[bank[pb], b_sg[1][ch]], wr=[b_sg[1][ch]])
                                k.op(DVE, lambda ch=ch, gch=gch, half=half: nc.vector.tensor_tensor(
                                    out=mergedT[:, gch, half * 512:(half + 1) * 512], in0=sg[:, 1, ch, half * 512:(half + 1) * 512],
                                    in1=m1[:, ch, half * 512:(half + 1) * 512], op=ALU.add),
                                    rd=[b_sg[1][ch], b_m1[ch]], wr=[b_mg[gch]])
                    ws.release()
            k.barrier()
            A.free("sg", "m1", "vT", "attnT", "h1T")

        rT = sbB("rT", [128, KC, T], F32); b_rT = [Buf(f"rT{i}") for i in range(KC)]
        xt = sbB("xt2", [128, 2, D], F32); b_xt = [Buf("xt2_0"), Buf("xt2_1")]
        s_xt = [k.new_sem("xt2_0"), k.new_sem("xt2_1")]
        for tile in range(8):
            so, tl = divmod(tile, 2)
            row0 = (4 * so + 3) * BLK + tl * 128
            xi = tile % 2
            k.dma(SP, s_xt[xi], xt[:, xi, :], x_all[row0:row0 + 128, :], wr=[b_xt[xi]])
            for g4 in range(4):
                pb = (tile * 4 + g4) % 2
                for q in range(4):
                    kc = g4 * 4 + q
                    k.op(PE, lambda xi=xi, kc=kc, q=q, pb=pb: nc.tensor.transpose(psum[:, pb, q * 128:(q + 1) * 128], xt[:, xi, kc * 128:(kc + 1) * 128], ident[:]),
                         rd=[b_xt[xi], b_ident], wr=[bank[pb]])
                k.op(ACT, lambda g4=g4, tile=tile, pb=pb: nc.scalar.copy(out=rT[:, g4 * 4:g4 * 4 + 4, tile * 128:(tile + 1) * 128],
                                                                         in_=psum[:, pb, :].rearrange("p (q t) -> p q t", q=4)),
                     rd=[bank[pb]], wr=b_rT[g4 * 4:g4 * 4 + 4])
        for cb in range(4):
            wb, wt = ws.acquire(f"wo{cb}")
            for ch in range(4):
                gch = cb * 4 + ch
                for half in range(2):
                    pb = 2 + (ch * 2 + half) % 2
                    mm_group(pb, 512, [(wt[:, kc, ch * 128:(ch + 1) * 128], mergedT[:, kc, half * 512:(half + 1) * 512]) for kc in range(KC)],
                             rd=[wb] + b_mg)
                    k.op(DVE, lambda gch=gch, half=half, pb=pb: nc.vector.tensor_tensor(
                        out=rT[:, gch, half * 512:(half + 1) * 512], in0=psum[:, pb, :], in1=rT[:, gch, half * 512:(half + 1) * 512], op=ALU.add),
                        rd=[bank[pb], b_rT[gch]], wr=[b_rT[gch]])
            ws.release()
        k.barrier()
        A.free("mergedT", "xt2")

        if dbg == "B":
            dt_ = sbB("dbgt", [128, 2048], F32); bd = Buf("dbgt")
            k.op(DVE, lambda: nc.vector.tensor_copy(out=dt_[:, 0:1024], in_=rT[:, 0, :]), rd=b_rT, wr=[bd])
            k.op(DVE, lambda: nc.vector.tensor_copy(out=dt_[:, 1024:2048], in_=rT[:, 5, :]), rd=b_rT, wr=[bd])
            k.dma(SP, k.new_sem("dbg"), dbg_out[:, 0:2048], dt_[:], rd=[bd])
            k.barrier()
            return

        hT = sbB("hT", [128, KC, T], BF16); b_hT = [Buf(f"hT{i}") for i in range(KC)]
        hid = sbB("hid", [128, 2, 8, T], BF16); b_hid = [[Buf(f"hid{a_}_{i}") for i in range(8)] for a_ in range(2)]
        tsq = sbB("tsq", [128, 2, 512], F32); b_tsq = [Buf("tsq0"), Buf("tsq1")]
        for half in range(2):
            hs = slice(half * 512, (half + 1) * 512)
            fm_norm((sq, b_sq), None, lambda kc: rT[:, kc, hs], b_rT, 512, KC, 1.0 / D,
                    lambda kc: gains_sb[:, 1, kc:kc + 1], lambda kc: hT[:, kc, hs], b_hT, 7, rs, b_rs)
        ti = 0
        ui = 0
        di = 0
        for j in range(8):
            hb = j % 2
            for cbh in range(2):
                wb, wt = ws.acquire(f"wup{j}_{cbh}")
                for ch in range(4):
                    lch = cbh * 4 + ch
                    for half in range(2):
                        hs = slice(half * 512, (half + 1) * 512)
                        pb = ui % 4
                        ui += 1
                        mm_group(pb, 512, [(wt[:, kc, ch * 128:(ch + 1) * 128], hT[:, kc, hs]) for kc in range(KC)], rd=[wb] + b_hT)
                        tq = ti % 2
                        ti += 1
                        k.op(ACT, lambda tq=tq, pb=pb: nc.scalar.activation(out=tsq[:, tq, :], in_=psum[:, pb, :], func=AF.Square),
                             rd=[bank[pb]], wr=[b_tsq[tq]])
                        k.op(DVE, lambda tq=tq, pb=pb, lch=lch, hb=hb, hs=hs: nc.vector.scalar_tensor_tensor(
                            out=hid[:, hb, lch, hs], in0=psum[:, pb, :], scalar=0.0, in1=tsq[:, tq, :], op0=ALU.is_gt, op1=ALU.mult),
                            rd=[bank[pb], b_tsq[tq]], wr=[b_hid[hb][lch]])
                ws.release()
            for cbh in range(2):
                wb, wt = ws.acquire(f"wdn{j}_{cbh}")
                for cc in range(8):
                    gch = cbh * 8 + cc
                    for half in range(2):
                        hs = slice(half * 512, (half + 1) * 512)
                        pb = 4 + di % 3
                        di += 1
                        mm_group(pb, 512, [(wt[:, kc, cc * 128:(cc + 1) * 128], hid[:, hb, kc, hs]) for kc in range(8)], rd=[wb] + b_hid[hb])
                        k.op(DVE, lambda gch=gch, pb=pb, hs=hs: nc.vector.tensor_tensor(out=rT[:, gch, hs], in0=psum[:, pb, :], in1=rT[:, gch, hs], op=ALU.add),
                             rd=[bank[pb], b_rT[gch]], wr=[b_rT[gch]])
                ws.release()
        k.barrier()
        A.free("hid", "tsq")
        if dbg == "C":
            dt_ = sbB("dbgt", [128, 2048], F32); bd = Buf("dbgt")
            k.op(DVE, lambda: nc.vector.tensor_copy(out=dt_[:, 0:1024], in_=rT[:, 0, :]), rd=b_rT, wr=[bd])
            k.op(DVE, lambda: nc.vector.tensor_copy(out=dt_[:, 1024:2048], in_=rT[:, 5, :]), rd=b_rT, wr=[bd])
            k.dma(SP, k.new_sem("dbg"), dbg_out[:, 0:2048], dt_[:], rd=[bd])
            k.barrier()
            A.free("dbgt")

        A.free("hT")
        h3T = sbB("h3T", [128, KC, T], BF16); b_h3T = [Buf(f"h3T{i}") for i in range(KC)]
        pT = sbB("pT", [128, 2, T], BF16); b_pT = [Buf("pT0"), Buf("pT1")]
        pt = sbB("pt", [128, 8, PLE], F32); b_pt = Buf("pt")
        sgp = sbB("sgp", [128, 2, 512], F32); b_sgp = [Buf("sgp0"), Buf("sgp1")]
        tmp = sbB("tmpp", [128, 2, 512], F32); b_tmp = [Buf("tmpp0"), Buf("tmpp1")]
        k.dma(SP, k.new_sem("pt"), pt[:], p_own.rearrange("(t p) c -> p t c", p=128), wr=[b_pt])
        for tile in range(8):
            pb = tile % 2
            for c2 in range(2):
                k.op(PE, lambda tile=tile, c2=c2, pb=pb: nc.tensor.transpose(psum[:, pb, c2 * 128:(c2 + 1) * 128], pt[:, tile, c2 * 128:(c2 + 1) * 128], ident[:]),
                     rd=[b_pt, b_ident], wr=[bank[pb]])
            k.op(ACT, lambda tile=tile, pb=pb: nc.scalar.copy(out=pT[:, :, tile * 128:(tile + 1) * 128],
                                                              in_=psum[:, pb, 0:256].rearrange("p (c t) -> p c t", c=2)),
                 rd=[bank[pb]], wr=b_pT)
        for half in range(2):
            hs = slice(half * 512, (half + 1) * 512)
            fm_norm((sq, b_sq), None, lambda kc: rT[:, kc, hs], b_rT, 512, KC, 1.0 / D,
                    lambda kc: gains_sb[:, 2, kc:kc + 1], lambda kc: h3T[:, kc, hs], b_h3T, 7, rs, b_rs)
        wtp = sbB("wppb", [128, 2, D], BF16); wbp = Buf("wppb")
        k.dma(POOL, k.new_sem("wppb"), wtp[:], st["w_pp"].rearrange("(k p) c -> p k c", p=128), wr=[wbp])
        gi = 0
        for cb in range(4):
            wb, wt = ws.acquire(f"wpg{cb}")
            for ch in range(4):
                gch = cb * 4 + ch
                for half in range(2):
                    hs = slice(half * 512, (half + 1) * 512)
                    pb = gi % 2
                    pb2 = 2 + gi % 2
                    g2 = gi % 2
                    gi += 1
                    mm_group(pb, 512, [(wt[:, kc, ch * 128:(ch + 1) * 128], h3T[:, kc, hs]) for kc in range(KC)], rd=[wb] + b_h3T)
                    k.op(ACT, lambda g2=g2, pb=pb: nc.scalar.activation(out=sgp[:, g2, :], in_=psum[:, pb, :], func=AF.Sigmoid),
                         rd=[bank[pb]], wr=[b_sgp[g2]])
                    mm_group(pb2, 512, [(wtp[:, kc, gch * 128:(gch + 1) * 128], pT[:, kc, hs]) for kc in range(2)], rd=[wbp] + b_pT)
                    k.op(DVE, lambda g2=g2, pb2=pb2: nc.vector.tensor_tensor(out=tmp[:, g2, :], in0=psum[:, pb2, :], in1=sgp[:, g2, :], op=ALU.mult),
                         rd=[bank[pb2], b_sgp[g2]], wr=[b_tmp[g2]])
                    k.op(DVE, lambda g2=g2, gch=gch, hs=hs: nc.vector.tensor_tensor(out=rT[:, gch, hs], in0=tmp[:, g2, :], in1=rT[:, gch, hs], op=ALU.add),
                         rd=[b_tmp[g2], b_rT[gch]], wr=[b_rT[gch]])
            ws.release()
        k.barrier()
        A.free("h3T", "pT", "pt", "sgp", "tmpp", "wppb")
        if dbg == "C":
            dt_ = sbB("dbgt", [128, 2048], F32); bd = Buf("dbgt")
            k.op(DVE, lambda: nc.vector.tensor_copy(out=dt_[:, 0:1024], in_=rT[:, 0, :]), rd=b_rT, wr=[bd])
            k.op(DVE, lambda: nc.vector.tensor_copy(out=dt_[:, 1024:2048], in_=rT[:, 5, :]), rd=b_rT, wr=[bd])
            k.dma(SP, k.new_sem("dbg"), dbg_out[:, 2048:4096], dt_[:], rd=[bd])
            k.barrier()
            A.free("dbgt")

        ot = sbB("ot", [128, 2, D], F32); b_ot = [Buf("ot0"), Buf("ot1")]
        s_ot = [k.new_sem("ot0"), k.new_sem("ot1")]
        for tile in range(8):
            oi = tile % 2
            for g4 in range(4):
                pb = (tile * 4 + g4) % 2
                for q in range(4):
                    kc = g4 * 4 + q
                    k.op(PE, lambda kc=kc, q=q, pb=pb, tile=tile: nc.tensor.transpose(psum[:, pb, q * 128:(q + 1) * 128], rT[:, kc, tile * 128:(tile + 1) * 128], ident[:]),
                         rd=[b_rT[kc], b_ident], wr=[bank[pb]])
                eng = ACT if g4 % 2 == 0 else DVE
                if eng is ACT:
                    k.op(ACT, lambda oi=oi, g4=g4, pb=pb: nc.scalar.copy(out=ot[:, oi, g4 * 512:(g4 + 1) * 512], in_=psum[:, pb, :]), rd=[bank[pb]], wr=[b_ot[oi]])
                else:
                    k.op(DVE, lambda oi=oi, g4=g4, pb=pb: nc.vector.tensor_copy(out=ot[:, oi, g4 * 512:(g4 + 1) * 512], in_=psum[:, pb, :]), rd=[bank[pb]], wr=[b_ot[oi]])
            k.dma(SP, s_ot[oi], out[tile * 128:(tile + 1) * 128, :], ot[:, oi, :], rd=[b_ot[oi]])
        k.barrier()


def make_inputs(x, p, g_mix, w_in, g_q, g_k, w_conv, w_attn_out, w_conv_out, w_o,
                g_mlp, w_up, w_down, g_ple, w_ple_gate, w_ple_proj):
    f = np.float32
    shared = {
        "w_in": np.ascontiguousarray(w_in[0], f),
        "w_conv": np.ascontiguousarray(np.asarray(w_conv[0], f).T.reshape(8, 128, 3).transpose(1, 0, 2)),
        "w_ao": np.ascontiguousarray(w_attn_out[0], f),
        "w_co": np.ascontiguousarray(w_conv_out[0], f),
        "w_o": np.ascontiguousarray(w_o[0], f),
        "w_up": np.ascontiguousarray(w_up[0], f),
        "w_down": np.ascontiguousarray(w_down[0], f),
        "w_pg": np.ascontiguousarray(w_ple_gate[0], f),
        "w_pp": np.ascontiguousarray(w_ple_proj[0], f),
        "gains": np.ascontiguousarray(np.stack([np.asarray(g, f)[0].reshape(KC, 128).T for g in (g_mix, g_mlp, g_ple)], axis=1)),
        "gqk": np.ascontiguousarray(np.stack([np.asarray(g_q, f)[0], np.asarray(g_k, f)[0]], axis=1)),
        "ident": np.eye(128, dtype=f),
    }
    esel = np.zeros((128, 16 * 128), f)
    for n in range(16):
        esel[n, n * 128:(n + 1) * 128] = 1.0
    shared["esel"] = esel.astype(ml_dtypes.bfloat16)
    tri = np.zeros((128, 2, 256), f)
    for kt in range(2):
        kk = kt * 128 + np.arange(128)[:, None]
        qq = np.arange(256)[None, :]
        tri[:, kt, :] = np.where(kk <= qq, 0.0, NEG)
    shared["tri"] = tri.astype(ml_dtypes.bfloat16)
    x = np.asarray(x, f)
    p = np.asarray(p, f)
    in_maps = []
    metas = []
    for r in range(8):
        b, j = divmod(r, 4)
        own, perm = core_perm(j)
        xb = x[b].reshape(NBLK, BLK, D)
        m = dict(shared)
        m["x_all"] = np.ascontiguousarray(xb[perm].reshape(S, D))
        halo = np.zeros((8, D), f)
        for s in range(4):
            g = own[s]
            if g > 0:
                halo[2 * s:2 * s + 2] = xb[g - 1, BLK - 2:BLK]
        m["x_halo"] = halo
        m["p_own"] = np.ascontiguousarray(p[0, b].reshape(NBLK, BLK, PLE)[own].reshape(T, PLE))
        pb = np.zeros((2, 8, 16), f)
        pb[0] = -BIG
        pb[1] = NEG
        for tt in range(8):
            s = tt // 2
            for pos in range(4 * s + 3):
                if perm[pos] < own[s]:
                    pb[0, tt, pos] = 0.0
                    pb[1, tt, pos] = 0.0
        m["pastb"] = np.ascontiguousarray(np.broadcast_to(pb[None], (128, 2, 8, 16)))
        in_maps.append(m)
        metas.append((b, own))
    return in_maps, metas


_NC_CACHE = {}


def kernel(**inputs):
    in_maps, metas = make_inputs(**inputs)
    if "nc" not in _NC_CACHE:
        _NC_CACHE["nc"] = build_program(DEBUG)
    nc = _NC_CACHE["nc"]
    in_maps = [{n: m[n] for n in nc._mk_declared} for m in in_maps]
    res = run_bass_kernel_spmd(nc, in_maps, core_ids=list(range(8)))
    outp = np.zeros((2, S, D), np.float32)
    for r in range(8):
        b, own = metas[r]
        o = np.asarray(res.results[r]["out"], np.float32).reshape(4, BLK, D)
        for s in range(4):
            outp[b, own[s] * BLK:(own[s] + 1) * BLK] = o[s]
    if DEBUG:
        kernel.dbg = [np.asarray(res.results[r]["dbg"]) for r in range(8)]
    return outp
```
